# Optimizing a Trainium2 kernel written in Bass

```python
import jax, jax.numpy as jnp
from jax import lax
import numpy as np

D_MODEL = 1024
BATCH = 8
SEQ = 4096
DEPTH = 2

PLE_DIM = 256
MIX_W = 512
N_BRANCH = 4
POOL_WINDOWS = (2, 4, 8, 16)
POOL_GROUP = MIX_W // len(POOL_WINDOWS)
RWKV_HEAD = 64
RWKV_HEADS = MIX_W // RWKV_HEAD
RWKV_DECAY_RANK = 64
RWKV_A_RANK = 64
RWKV_G_RANK = 128
RWKV_LN_EPS = 64e-5
RWKV_IN = 3 * MIX_W + RWKV_DECAY_RANK + RWKV_A_RANK + RWKV_G_RANK
RWKV_SPLITS = [MIX_W, 2 * MIX_W, 3 * MIX_W, 3 * MIX_W + RWKV_DECAY_RANK,
               3 * MIX_W + RWKV_DECAY_RANK + RWKV_A_RANK]
SG_CHUNK = 128
SG_GROUPS = 4
SG_GROUP_W = MIX_W // SG_GROUPS
SG_LN_EPS = 1e-5
HGRN_EXPAND = 128
HGRN_HEADS = MIX_W // HGRN_EXPAND
HGRN_CHUNK = 64
GATE_FLOOR = 1e-30
D_FF = 2816
CONV_W = 3
NORM_EPS = 1e-6
COL_A = MIX_W
COL_B = RWKV_IN
COL_C = 2 * MIX_W
COL_D = 3 * MIX_W
COL_G = N_BRANCH * D_MODEL
D_IN = COL_A + COL_B + COL_C + COL_D + COL_G
IN_SPLITS = [COL_A, COL_A + COL_B, COL_A + COL_B + COL_C, COL_A + COL_B + COL_C + COL_D]

kernel_name = "hybrid_pool_rwkv7_gmlp_hgrn2_block"


def rmsnorm(x, g, eps=NORM_EPS):
    x32 = x.astype(jnp.float32)
    y = x32 * lax.rsqrt(jnp.mean(x32 * x32, axis=-1, keepdims=True) + eps)
    return (y * g.astype(jnp.float32)).astype(x.dtype)


def shift_right(x, n=1):
    return jnp.pad(x, ((0, 0), (n, 0), (0, 0)))[:, :x.shape[1]]


def pool_mixer(a, w_group, scale):
    B, S, _ = a.shape
    a32 = a.astype(jnp.float32)
    csum = jnp.cumsum(a32, axis=1)
    count = jnp.arange(1, S + 1, dtype=jnp.float32)[None, :, None]
    diffs = []
    for gi, win in enumerate(POOL_WINDOWS):
        c = csum[..., gi * POOL_GROUP:(gi + 1) * POOL_GROUP]
        mean = (c - shift_right(c, win)) / jnp.minimum(count, win)
        diffs.append(mean - a32[..., gi * POOL_GROUP:(gi + 1) * POOL_GROUP])
    d = jnp.stack(diffs, axis=2).astype(a.dtype)
    y = jnp.einsum('bsgc,gcd->bsgd', d, w_group).reshape(B, S, MIX_W)
    return (y * scale).astype(a.dtype)


def rwkv7_mixer(xb, mu, w0, w2, a0, a2, g2, k_k, k_a, r_k, ln_w, ln_b):
    B, S, _ = xb.shape
    f32 = jnp.float32
    xb = xb + mu * (shift_right(xb) - xb)
    r, k, v, lw, la, lg = jnp.split(xb, RWKV_SPLITS, axis=-1)
    w = -jax.nn.softplus(-(w0 + jnp.tanh(lw) @ w2).astype(f32)) - 0.5
    decay = jnp.exp(-jnp.exp(w))
    a = jax.nn.sigmoid((a0 + la @ a2).astype(f32))
    g = jax.nn.sigmoid(lg) @ g2
    hd = lambda t: t.reshape(B, S, RWKV_HEADS, RWKV_HEAD)
    kk = hd((k * k_k).astype(f32))
    kk = kk / jnp.maximum(jnp.linalg.norm(kk, axis=-1, keepdims=True), 1e-12)
    k = k.astype(f32) * (1 + (a - 1) * k_a)
    r4, k4, v4, a4, w4 = hd(r.astype(f32)), hd(k), hd(v.astype(f32)), hd(a), hd(decay)

    def step(state, inp):
        r_t, w_t, k_t, v_t, kk_t, a_t = inp
        sa = jnp.einsum('bhvk,bhk->bhv', state, -kk_t)
        state = (state * w_t[:, :, None, :] + sa[..., None] * (kk_t * a_t)[:, :, None, :]
                 + v_t[..., None] * k_t[:, :, None, :])
        return state, jnp.einsum('bhvk,bhk->bhv', state, r_t)

    tm = lambda t: jnp.moveaxis(t, 1, 0)
    s0 = jnp.zeros((B, RWKV_HEADS, RWKV_HEAD, RWKV_HEAD), f32)
    _, y = lax.scan(step, s0, (tm(r4), tm(w4), tm(k4), tm(v4), tm(kk), tm(a4)))
    y = jnp.moveaxis(y, 0, 1)
    mean = jnp.mean(y, axis=-1, keepdims=True)
    var = jnp.var(y, axis=-1, keepdims=True)
    y = ((y - mean) * lax.rsqrt(var + RWKV_LN_EPS)).reshape(B, S, MIX_W) * ln_w + ln_b
    bonus = jnp.sum(r4 * k4 * r_k, axis=-1, keepdims=True) * v4
    y = y + bonus.reshape(B, S, MIX_W)
    return (y * g).astype(xb.dtype)


def spatial_gating_mixer(xc, ln_w, ln_b, w_s, b_s):
    B, S, _ = xc.shape
    z = jax.nn.gelu(xc)
    u, v = jnp.split(z, 2, axis=-1)
    v32 = v.astype(jnp.float32)
    mean = jnp.mean(v32, axis=-1, keepdims=True)
    var = jnp.var(v32, axis=-1, keepdims=True)
    v = ((v32 - mean) * lax.rsqrt(var + SG_LN_EPS) * ln_w + ln_b).astype(xc.dtype)
    mask = jnp.tril(jnp.ones((SG_CHUNK, SG_CHUNK), dtype=bool))
    w_m = jnp.where(mask, w_s, 0)
    vc = v.reshape(B, S // SG_CHUNK, SG_CHUNK, SG_GROUPS, SG_GROUP_W)
    mixed = jnp.einsum('gts,bnsgc->bntgc', w_m, vc) + jnp.swapaxes(b_s, 0, 1)[None, None, :, :, None]
    return (u * mixed.reshape(B, S, MIX_W)).astype(xc.dtype)


def hgrn2_mixer(xd, lb, norm_w):
    B, S, _ = xd.shape
    f32 = jnp.float32
    q, f, i = jnp.split(xd, 3, axis=-1)
    q = jax.nn.silu(q.astype(f32))
    f = f.astype(f32)
    lb = lb.astype(f32)
    sig = jax.nn.sigmoid(f)
    log_g = jnp.log(jnp.maximum(lb + (1 - lb) * sig, GATE_FLOOR))
    key = (1 - lb) * (1 - sig)
    NC = S // HGRN_CHUNK
    ch = lambda t: t.reshape(B, NC, HGRN_CHUNK, HGRN_HEADS, HGRN_EXPAND).transpose(1, 0, 3, 2, 4)
    mask = jnp.tril(jnp.ones((HGRN_CHUNK, HGRN_CHUNK), dtype=bool))[:, :, None]

    def chunk_step(state, inp):
        q_c, k_c, v_c, lg_c = inp
        cum = jnp.cumsum(lg_c, axis=2)
        o_inter = jnp.einsum('bhtk,bhkv->bhtv', q_c * jnp.exp(cum), state)
        rel = cum[:, :, :, None, :] - cum[:, :, None, :, :]
        dec = jnp.where(mask, jnp.exp(jnp.where(mask, rel, 0.0)), 0.0)
        scores = jnp.einsum('bhtk,bhtsk,bhsk->bhts', q_c, dec, k_c)
        o_intra = jnp.einsum('bhts,bhsv->bhtv', scores, v_c)
        last = cum[:, :, -1:, :]
        state = (jnp.exp(last[:, :, 0, :, None]) * state
                 + jnp.einsum('bhsk,bhsv->bhkv', k_c * jnp.exp(last - cum), v_c))
        return state, o_inter + o_intra

    s0 = jnp.zeros((B, HGRN_HEADS, HGRN_EXPAND, HGRN_EXPAND), f32)
    _, o = lax.scan(chunk_step, s0, (ch(q), ch(key), ch(i.astype(f32)), ch(log_g)))
    o = o.transpose(1, 0, 3, 2, 4).reshape(B, S, HGRN_HEADS, HGRN_EXPAND)
    o = o * lax.rsqrt(jnp.mean(o * o, axis=-1, keepdims=True) + NORM_EPS)
    return (o.reshape(B, S, MIX_W) * norm_w).astype(xd.dtype)


def conv_glu_ffn(h, w_up, conv_w, conv_b, w_down):
    u = h @ w_up
    u = conv_b + sum(conv_w[CONV_W - 1 - j] * shift_right(u, j) for j in range(CONV_W))
    gate, val = jnp.split(u, 2, axis=-1)
    return (jax.nn.gelu(gate) * val) @ w_down


def setup_inputs(seed: int = 0) -> dict:
    key = jax.random.key(seed)
    ks = iter(jax.random.split(key, 40))
    nrm = lambda shape, s: jax.random.normal(next(ks), shape, jnp.float32) * s
    gain = lambda shape: 1.0 + nrm(shape, 0.02)
    L = DEPTH
    return {
        "x": nrm((BATCH, SEQ, D_MODEL), 1.0),
        "p": nrm((L, BATCH, SEQ, PLE_DIM), 1.0),
        "norm_mix": gain((L, D_MODEL)),
        "w_in": nrm((L, D_MODEL, D_IN), D_MODEL ** -0.5),
        "pool_w": nrm((L, len(POOL_WINDOWS), POOL_GROUP, POOL_GROUP), POOL_GROUP ** -0.5),
        "pool_scale": gain((L, MIX_W)),
        "rwkv_mu": jax.random.uniform(next(ks), (L, RWKV_IN), jnp.float32),
        "rwkv_w0": jax.random.uniform(next(ks), (L, MIX_W), jnp.float32, -6.0, -1.0),
        "rwkv_w2": nrm((L, RWKV_DECAY_RANK, MIX_W), 0.1 * RWKV_DECAY_RANK ** -0.5),
        "rwkv_a0": nrm((L, MIX_W), 0.1),
        "rwkv_a2": nrm((L, RWKV_A_RANK, MIX_W), RWKV_A_RANK ** -0.5),
        "rwkv_g2": nrm((L, RWKV_G_RANK, MIX_W), RWKV_G_RANK ** -0.5),
        "rwkv_kk": 0.85 + nrm((L, MIX_W), 0.02),
        "rwkv_ka": gain((L, MIX_W)),
        "rwkv_rk": nrm((L, RWKV_HEADS, RWKV_HEAD), 0.1),
        "rwkv_ln_w": gain((L, MIX_W)),
        "rwkv_ln_b": nrm((L, MIX_W), 0.02),
        "sg_ln_w": gain((L, MIX_W)),
        "sg_ln_b": nrm((L, MIX_W), 0.02),
        "sg_w": nrm((L, SG_GROUPS, SG_CHUNK, SG_CHUNK), SG_CHUNK ** -0.5),
        "sg_b": gain((L, SG_GROUPS, SG_CHUNK)),
        "hgrn_lb": nrm((L, MIX_W), 0.5),
        "hgrn_norm": gain((L, MIX_W)),
        "w_branch": nrm((L, N_BRANCH, MIX_W, D_MODEL), MIX_W ** -0.5),
        "w_out": nrm((L, D_MODEL, D_MODEL), D_MODEL ** -0.5),
        "norm_ffn": gain((L, D_MODEL)),
        "ffn_up": nrm((L, D_MODEL, 2 * D_FF), D_MODEL ** -0.5),
        "ffn_conv": nrm((L, CONV_W, 2 * D_FF), CONV_W ** -0.5),
        "ffn_conv_b": nrm((L, 2 * D_FF), 0.02),
        "ffn_down": nrm((L, D_FF, D_MODEL), D_FF ** -0.5),
        "norm_ple": gain((L, D_MODEL)),
        "ple_proj": nrm((L, PLE_DIM, D_MODEL), PLE_DIM ** -0.5),
        "ple_gate": nrm((L, D_MODEL, D_MODEL), D_MODEL ** -0.5),
        "norm_final": gain((D_MODEL,)),
    }


def reference(x, p, norm_mix, w_in, pool_w, pool_scale, rwkv_mu, rwkv_w0, rwkv_w2, rwkv_a0, rwkv_a2,
              rwkv_g2, rwkv_kk, rwkv_ka, rwkv_rk, rwkv_ln_w, rwkv_ln_b, sg_ln_w, sg_ln_b, sg_w, sg_b,
              hgrn_lb, hgrn_norm, w_branch, w_out, norm_ffn, ffn_up, ffn_conv, ffn_conv_b, ffn_down,
              norm_ple, ple_proj, ple_gate, norm_final):
    B, S, _ = x.shape
    probs = jax.nn.softmax(hgrn_lb.astype(jnp.float32), axis=0)
    lower_bounds = jnp.cumsum(probs, axis=0) - probs[0]
    for l in range(DEPTH):
        h = rmsnorm(x, norm_mix[l])
        proj = h @ w_in[l]
        xa, xb, xc, xd, g = jnp.split(proj, IN_SPLITS, axis=-1)
        branches = (
            pool_mixer(xa, pool_w[l], pool_scale[l]),
            rwkv7_mixer(xb, rwkv_mu[l], rwkv_w0[l], rwkv_w2[l], rwkv_a0[l], rwkv_a2[l], rwkv_g2[l],
                        rwkv_kk[l], rwkv_ka[l], rwkv_rk[l], rwkv_ln_w[l], rwkv_ln_b[l]),
            spatial_gating_mixer(xc, sg_ln_w[l], sg_ln_b[l], sg_w[l], sg_b[l]),
            hgrn2_mixer(xd, lower_bounds[l], hgrn_norm[l]),
        )
        gates = jax.nn.sigmoid(g.reshape(B, S, N_BRANCH, D_MODEL))
        merged = sum(gates[:, :, k] * (branches[k] @ w_branch[l, k]) for k in range(N_BRANCH))
        x = x + merged @ w_out[l]
        x = x + conv_glu_ffn(rmsnorm(x, norm_ffn[l]), ffn_up[l], ffn_conv[l], ffn_conv_b[l], ffn_down[l])
        x = x + (p[l] @ ple_proj[l]) * jax.nn.sigmoid(rmsnorm(x, norm_ple[l]) @ ple_gate[l])
    return rmsnorm(x, norm_final)
```

```python
import contextlib
import numpy as np
import concourse.bass as bass
import concourse.mybir as mybir
from concourse.bass_utils import run_bass_kernel_spmd

F32 = mybir.dt.float32
BF16 = mybir.dt.bfloat16
F32R = mybir.dt.float32r
RDT = BF16
AF = mybir.ActivationFunctionType
ALU = mybir.AluOpType

D = 1024
DEPTH = 2
MIXW = 512
D_IN = 8960
DFF = 2816
PLE = 256
C_A, C_B, C_C, C_D, C_G = 0, 512, 2304, 3328, 4864
LAMB = float(np.exp(-0.5))


class KB:
    def __init__(self, nc, n_dma_sems=32):
        self.nc = nc
        self.es = contextlib.ExitStack()
        self.eng = {"pe": nc.tensor, "dve": nc.vector, "act": nc.scalar, "pool": nc.gpsimd, "sp": nc.sync}
        self.sem = {}
        self.cnt = {}
        for name in self.eng:
            self.sem[name] = self.es.enter_context(nc.semaphore("s_" + name))
            self.cnt[name] = 0
        self.dma_sems = [self.es.enter_context(nc.semaphore("d%d" % i)) for i in range(2 * n_dma_sems)]
        self.dma_val = [0] * (2 * n_dma_sems)
        self.n_dma_sems = n_dma_sems
        self.dma_rr = {"hw": 0, "sw": 0}
        self.seen = {name: {} for name in self.eng}
        self.last_w = {}
        self.readers = {}
        self.n_inst = 0
        self.n_wait = 0
        self.alias = {}

    def _semh(self, semkey):
        if isinstance(semkey, tuple):
            return self.dma_sems[semkey[1]]
        return self.sem[semkey]

    def _wait(self, e, tok):
        semkey, val = tok
        if self.seen[e].get(semkey, 0) >= val:
            return
        self.eng[e].wait_ge(self._semh(semkey), val)
        self.seen[e][semkey] = val
        self.n_wait += 1

    def _deps(self, e, reads, writes, acc=False):
        toks = {}

        def add(tok):
            if tok is None:
                return
            k, v = tok
            if toks.get(k, 0) < v:
                toks[k] = v
        for k in reads:
            add(self.last_w.get(k))
        for k in writes:
            lw = self.last_w.get(k)
            if not (acc and lw is not None and lw[0] == e):
                add(lw)
            for t in self.readers.get(k, ()):
                add(t)
        for k, v in toks.items():
            self._wait(e, (k, v))

    def _commit(self, tok, reads, writes):
        for k in reads:
            lst = self.readers.setdefault(k, [])
            lst.append(tok)
            if len(lst) > 16:
                d = {}
                for (sk, v) in lst:
                    if d.get(sk, 0) < v:
                        d[sk] = v
                self.readers[k] = list(d.items())
        for k in writes:
            self.last_w[k] = tok
            self.readers[k] = []

    def _x(self, keys):
        al = self.alias
        if not al:
            return keys
        out = []
        for k in keys:
            out.extend(al.get(k, (k,)))
        return out

    def op(self, e, fn, reads=(), writes=(), acc=False):
        reads, writes = self._x(reads), self._x(writes)
        self._deps(e, reads, writes, acc)
        ins = fn(self.eng[e])
        self.cnt[e] += 1
        ins.then_inc(self.sem[e], 1)
        self._commit((e, self.cnt[e]), reads, writes)
        self.n_inst += 1
        return ins

    def dma(self, q, out, in_, reads=(), writes=(), **kw):
        reads, writes = self._x(reads), self._x(writes)
        self._deps(q, reads, writes)
        kind = "sw" if q == "pool" else "hw"
        i = self.dma_rr[kind] + (self.n_dma_sems if kind == "sw" else 0)
        self.dma_rr[kind] = (self.dma_rr[kind] + 1) % self.n_dma_sems
        if self.dma_val[i] > 0:
            self._wait(q, (("d", i), self.dma_val[i]))
        ins = self.eng[q].dma_start(out=out, in_=in_, **kw)
        self.dma_val[i] += 16
        ins.then_inc(self.dma_sems[i], 16)
        tok = (("d", i), self.dma_val[i])
        self._commit(tok, reads, writes)
        self.n_inst += 1
        return tok

    def wait_all(self, e):
        toks = {}
        for tok in self.last_w.values():
            k, v = tok
            if toks.get(k, 0) < v:
                toks[k] = v
        for k, v in toks.items():
            self._wait(e, (k, v))

    def close(self):
        self.es.close()


CST_COLS = 128 * 8 + 64


def make_consts():
    i = np.arange(128)[:, None]
    j = np.arange(128)[None, :]
    same = (i // 32) == (j // 32)
    mats = [
        (i == j),
        (i < j),
        (i <= j),
        (i > j),
        (i <= j) & same,
        (i > j) & same,
        (i // 64) == (j // 64),
        np.ones((128, 128), bool),
    ]
    c = np.concatenate([m.astype(np.float32) for m in mats], axis=1)
    fix = np.zeros((128, 4, 16), np.float32)
    for g, w in enumerate((2, 4, 8, 16)):
        t = np.arange(16)
        fix[:, g, :] = (w / np.minimum(t + 1, w))[None, :]
    return np.ascontiguousarray(np.concatenate([c, fix.reshape(128, 64)], axis=1))


PV_L = {}
_o = 0
for _n, _w in [("g_mix", 8), ("g_ffn", 8), ("g_ple", 8), ("pool_scale", 4), ("mu", 14), ("w0", 4), ("a0", 4),
               ("kk", 4), ("ka", 4), ("rk", 4), ("ln_w", 4), ("ln_b", 4), ("hnorm", 4), ("cw0", 44), ("cw1", 44),
               ("cw2", 44), ("cb", 44), ("lb_a", 4), ("lb_b", 4)]:
    PV_L[_n] = (_o, _w)
    _o += _w
PV_LW = _o


def fm(v, n):
    return np.ascontiguousarray(np.asarray(v, np.float32).reshape(n, 128).T)


def pack_pv(inp):
    out = np.zeros((128, DEPTH * PV_LW), np.float32)
    for l in range(DEPTH):
        def put(name, arr):
            o, w = PV_L[name]
            out[:, l * PV_LW + o: l * PV_LW + o + w] = arr
        put("g_mix", fm(inp["norm_mix"][l], 8))
        put("g_ffn", fm(inp["norm_ffn"][l], 8))
        put("g_ple", fm(inp["norm_ple"][l], 8))
        put("pool_scale", fm(inp["pool_scale"][l], 4))
        put("mu", fm(inp["rwkv_mu"][l], 14))
        put("w0", fm(inp["rwkv_w0"][l], 4))
        put("a0", fm(inp["rwkv_a0"][l], 4))
        put("kk", fm(inp["rwkv_kk"][l], 4))
        put("ka", fm(inp["rwkv_ka"][l], 4))
        put("rk", fm(inp["rwkv_rk"][l].reshape(-1), 4))
        put("ln_w", fm(inp["rwkv_ln_w"][l], 4))
        put("ln_b", fm(inp["rwkv_ln_b"][l], 4))
        put("hnorm", fm(inp["hgrn_norm"][l], 4))
        for j in range(3):
            put("cw%d" % j, fm(inp["ffn_conv"][l, j], 44))
        put("cb", fm(inp["ffn_conv_b"][l], 44))
        put("lb_a", fm(inp["hgrn_lb"][0], 4))
        put("lb_b", fm(inp["hgrn_lb"][1], 4))
    return out


BR_W = 512 + 512
BC_COLS = DEPTH * BR_W + 1024


def pack_bc(inp):
    rows = []
    for l in range(DEPTH):
        rows.append(np.concatenate([inp["sg_ln_w"][l], inp["sg_ln_b"][l]]).astype(np.float32))
    rows.append(np.asarray(inp["norm_final"], np.float32))
    r = np.concatenate(rows)
    return np.ascontiguousarray(np.broadcast_to(r[None, :], (128, r.shape[0])))


def pack_sgb(inp):
    return np.ascontiguousarray(np.asarray(inp["sg_b"], np.float32).reshape(1, DEPTH * 512))


def pack_small(inp):
    outs = []
    for l in range(DEPTH):
        pw = np.transpose(inp["pool_w"][l], (1, 0, 2)).reshape(128, 512)
        sw = np.transpose(inp["sg_w"][l], (1, 0, 2)).reshape(128, 512)
        w2a2 = np.concatenate([inp["rwkv_w2"][l], inp["rwkv_a2"][l]], axis=0)
        g2 = inp["rwkv_g2"][l]
        outs.append(np.concatenate([pw, sw, w2a2, g2], axis=1))
    return np.ascontiguousarray(np.concatenate(outs, axis=1).astype(np.float32))


SM_W = 2048


def build_nc(S, TT=512, nlayers=DEPTH, debug=False, stages=("pool", "rwkv", "sg", "hgrn", "ffn", "ple")):
    NJ = TT // 128
    NT = S // TT
    nc = bass.Bass("TRN2", target_bir_lowering=False)
    dt = nc.dram_tensor
    x_d = dt("x", [S, D], F32, kind="ExternalInput").ap()
    p_d = dt("p", [DEPTH, S, PLE], F32, kind="ExternalInput").ap()
    w_in_d = dt("w_in", [DEPTH, D, D_IN], F32, kind="ExternalInput").ap()
    w_br_d = dt("w_branch", [DEPTH, 4 * MIXW, D], F32, kind="ExternalInput").ap()
    w_out_d = dt("w_out", [DEPTH, D, D], F32, kind="ExternalInput").ap()
    w_up_d = dt("ffn_up", [DEPTH, D, 2 * DFF], F32, kind="ExternalInput").ap()
    w_dn_d = dt("ffn_down", [DEPTH, DFF, D], F32, kind="ExternalInput").ap()
    w_pp_d = dt("ple_proj", [DEPTH, PLE, D], F32, kind="ExternalInput").ap()
    w_pg_d = dt("ple_gate", [DEPTH, D, D], F32, kind="ExternalInput").ap()
    cst_d = dt("cst", [128, CST_COLS], F32, kind="ExternalInput").ap()
    pv_d = dt("pv", [128, DEPTH * PV_LW], F32, kind="ExternalInput").ap()
    bc_d = dt("bc", [128, BC_COLS], F32, kind="ExternalInput").ap()
    sgb_d = dt("sgb", [1, DEPTH * 512], F32, kind="ExternalInput").ap()
    sm_d = dt("sm", [128, DEPTH * SM_W], F32, kind="ExternalInput").ap()
    out_d = dt("out", [S, D], F32, kind="ExternalOutput").ap()
    def _blocks():
        b = {}
        in_cols = [(C_A, 512)] + [(C_B + g * 512, 512) for g in range(3)] + [(C_B + 1536, 256)] + \
                  [(C_C, 512), (C_C + 512, 512)] + [(C_D + i * 512, 512) for i in range(3)] + \
                  [(C_G + i * 512, 512) for i in range(8)]
        b["in"] = [(0, 8, c0, w) for c0, w in in_cols]
        b["br"] = [(k * 512, 4, h * 512, 512) for k in range(4) for h in range(2)]
        b["out"] = [(0, 8, h * 512, 512) for h in range(2)]
        up = []
        for gq in range(6):
            nch = min(4, 22 - 4 * gq)
            up += [(0, 8, gq * 512, nch * 128), (0, 8, DFF + gq * 512, nch * 128)]
        b["up"] = up
        b["dn"] = [(kg * 1024, 8 if kg < 2 else 6, h * 512, 512) for h in range(2) for kg in range(3)]
        b["pp"] = [(0, 2, h * 512, 512) for h in range(2)]
        b["pg"] = [(0, 8, h * 512, 512) for h in range(2)]
        return b
    WBLK = _blocks()
    WOFF = {}
    WTOT = {}
    for _nm, _lst in WBLK.items():
        _o = 0
        for _i, (_r0, _nk, _c0, _w) in enumerate(_lst):
            WOFF[(_nm, _r0, _c0)] = (_o, _nk, _w, _i)
            _o += _nk * 128 * _w
        WTOT[_nm] = _o
    WSRC = {"in": w_in_d, "br": w_br_d, "out": w_out_d, "up": w_up_d, "dn": w_dn_d, "pp": w_pp_d, "pg": w_pg_d}
    SCR = {nm: dt("s_" + nm, [DEPTH, WTOT[nm]], BF16, kind="Internal").ap() for nm in WBLK}
    if debug:
        dbg_br = dt("dbg_br", [DEPTH, 4, 4, 128, S], F32, kind="ExternalOutput").ap()
        dbg_x = dt("dbg_x", [DEPTH, 3, S, D], F32, kind="ExternalOutput").ap()

    kb = KB(nc)
    es = kb.es

    def sb(name, shape, dtype=F32):
        return es.enter_context(nc.sbuf_tensor(name, shape, dtype))

    ARENA_WORDS = 33 * TT
    ARENA = es.enter_context(nc.sbuf_tensor("arena", [128, ARENA_WORDS], F32))
    aoff = [0]

    amax = [0]

    def arena_reset():
        amax[0] = max(amax[0], aoff[0])
        for e_ in ("pe", "dve", "act", "pool", "sp"):
            kb.wait_all(e_)
        aoff[0] = 0

    def ab(name, shape, dtype=F32):
        n = int(np.prod(shape[1:]))
        words = n if dtype in (F32, F32R) else (n + 1) // 2
        o = aoff[0]
        aoff[0] += words
        assert aoff[0] <= ARENA_WORDS, (name, aoff[0])
        v = ARENA[:, o:o + words]
        if dtype != F32:
            v = v.bitcast(dtype)
        if len(shape) == 2:
            return v
        if len(shape) == 3:
            v = v.rearrange("p (a b) -> p a b", a=shape[1])
        elif len(shape) == 4:
            v = v.rearrange("p (a b c) -> p a b c", a=shape[1], b=shape[2])
        return v

    PS = [es.enter_context(nc.psum_tensor("ps%d" % i, [128, 512], F32)) for i in range(8)]

    CST = sb("cst_t", [128, CST_COLS])
    PV = sb("pv_t", [128, DEPTH * PV_LW])
    BC = sb("bc_t", [128, BC_COLS])
    SMF = ab("smf_t", [128, SM_W])
    kb.dma("sp", CST[:], cst_d, writes=["cst"])
    kb.dma("sp", PV[:], pv_d, writes=["pv"])
    kb.dma("sp", BC[:], bc_d, writes=["bc"])
    IDENT = CST[:, 0:128]
    M_LT = CST[:, 128:256]
    M_LE = CST[:, 256:384]
    M_GT = CST[:, 384:512]
    MH_LE = CST[:, 512:640]
    MH_GT = CST[:, 640:768]
    BLK64 = CST[:, 768:896]
    ONES = CST[:, 896:1024]
    IDB = sb("idb", [128, 128], BF16)
    ONEB = sb("oneb", [128, 128], BF16)
    kb.op("dve", lambda e: e.tensor_copy(out=IDB[:], in_=IDENT), reads=["cst"], writes=["idb"])
    kb.op("dve", lambda e: e.tensor_copy(out=ONEB[:], in_=ONES), reads=["cst"], writes=["oneb"])
    BLKB = sb("blkb", [128, 128], BF16)
    kb.op("dve", lambda e: e.tensor_copy(out=BLKB[:], in_=BLK64), reads=["cst"], writes=["blkb"])

    def pv(l, name, c=None):
        o, w = PV_L[name]
        o += l * PV_LW
        if c is None:
            return PV[:, o:o + w]
        return PV[:, o + c:o + c + 1]

    for l in range(nlayers):
        for nm, lst in WBLK.items():
            for (r0, nk, c0, w) in lst:
                off, _, _, bi = WOFF[(nm, r0, c0)]
                dst = SCR[nm][l, off:off + nk * 128 * w].rearrange("(p k n) -> p k n", p=128, k=nk)
                src = WSRC[nm][l, r0:r0 + nk * 128, c0:c0 + w].rearrange("(k p) n -> p k n", p=128)
                kb.dma("pool", dst, src, writes=["W%s%d_%d" % (nm, l, bi)])

    NSLOT = 3
    WB = [sb("wb%d" % i, [128, 4096], BF16) for i in range(NSLOT)]
    wrr = [0]
    for _i in range(NSLOT):
        kb.alias["wb%d" % _i] = ("wb%da" % _i, "wb%db" % _i)

    def load_w(name, l, src3, kc, ncol, rows=None, slot=None):
        ap2, ckey = src3
        if slot is None:
            i = wrr[0]
            wrr[0] = (i + 1) % NSLOT
            o, key = 0, "wb%d" % i
        else:
            i, hf = slot
            o, key = (0, "wb%d" % i) if hf is None else (hf * 2048, "wb%d%s" % (i, "ab"[hf]))
        kb.dma("sp", WB[i][:, o:o + kc * ncol], ap2, reads=[ckey], writes=[key])
        view = WB[i][:, o:o + kc * ncol].rearrange("p (k n) -> p k n", k=kc)
        return view, key

    def wview(nm, l, r0, nk, c0, ncol):
        off, nk_, w_, bi = WOFF[(nm, r0, c0)]
        assert nk_ == nk and w_ == ncol, (nm, r0, c0, nk, ncol)
        ap2 = SCR[nm][l, off:off + nk * 128 * ncol].rearrange("(p x) -> p x", p=128)
        return ap2, "W%s%d_%d" % (nm, l, bi)

    X = sb("X", [128, NJ, D])
    HT = sb("HT", [128, 8, TT], BF16)
    XN = [sb("XN%d" % i, [128, D], BF16) for i in range(2)]
    SS = sb("ss", [128, 8])
    BR = [sb("BR%d" % k, [128, 4, TT], BF16) for k in range(4)]
    TMP = [sb("TMP%d" % i, [128, 512]) for i in range(2)]

    POOLW = sb("poolw", [128, DEPTH, 512], BF16)
    SGWT = sb("sgwt", [128, DEPTH, 512], BF16)
    W2A2 = sb("w2a2", [128, DEPTH, 512], BF16)
    G2 = sb("g2", [128, DEPTH, 512], BF16)
    SGB = sb("sgb_t", [128, DEPTH, 512], BF16)
    for l in range(nlayers):
        kb.dma("sp", SMF[:], sm_d[:, l * SM_W:(l + 1) * SM_W], writes=["smf"])
        kb.op("dve", lambda e: e.tensor_copy(out=POOLW[:, l, :], in_=SMF[:, 0:512]), reads=["smf"], writes=["poolw"])
        kb.op("dve", lambda e: e.tensor_copy(out=W2A2[:, l, :], in_=SMF[:, 1024:1536]), reads=["smf"], writes=["w2a2"])
        kb.op("dve", lambda e: e.tensor_copy(out=G2[:, l, :], in_=SMF[:, 1536:2048]), reads=["smf"], writes=["g2"])
        kb.dma("sp", SMF[0:1, 0:512], sgb_d[:, l * 512:(l + 1) * 512], reads=["poolw", "w2a2", "g2", "sgwt"], writes=["smf"])
        kb.op("dve", lambda e: e.tensor_copy(out=SGB[0:1, l, :], in_=SMF[0:1, 0:512]), reads=["smf"], writes=["sgb"])
        for g in range(4):
            kb.op("pe", lambda e: e.transpose(out=PS[0][:, 0:128], in_=SMF[:, 512 + g * 128:512 + (g + 1) * 128],
                                              identity=IDENT), reads=["smf", "cst"], writes=["ps0"])
            kb.op("dve", lambda e: e.tensor_tensor(out=SGWT[:, l, g * 128:(g + 1) * 128], in0=PS[0][:, 0:128],
                                                   in1=M_LE, op=ALU.mult), reads=["ps0", "cst"], writes=["sgwt"])

    LBT = sb("lbt", [128, DEPTH, 4])
    OML = sb("oml", [128, DEPTH, 4])
    OMKA = sb("omka", [128, DEPTH, 4])
    kb.op("dve", lambda e: e.tensor_tensor(out=LBT[:, 1, :], in0=pv(0, "lb_b"), in1=pv(0, "lb_a"), op=ALU.subtract),
          reads=["pv"], writes=["lbt"])
    kb.op("act", lambda e: e.activation(out=LBT[:, 1, :], in_=LBT[:, 1, :], func=AF.Sigmoid), reads=["lbt"], writes=["lbt"])
    kb.op("dve", lambda e: e.memset(LBT[:, 0, :], 0.0), reads=["lbt"], writes=["lbt"])
    kb.op("dve", lambda e: e.tensor_scalar(out=OML[:], in0=LBT[:], scalar1=-1.0, scalar2=1.0, op0=ALU.mult, op1=ALU.add),
          reads=["lbt"], writes=["oml"])
    for l in range(nlayers):
        kb.op("dve", lambda e: e.tensor_scalar(out=OMKA[:, l, :], in0=pv(l, "ka"), scalar1=-1.0, scalar2=1.0,
                                               op0=ALU.mult, op1=ALU.add), reads=["pv"], writes=["omka"])

    POOLH = sb("poolh", [128, DEPTH, 4, 16])
    kb.op("pool", lambda e: e.memset(POOLH[:], 0.0), writes=["poolh"])


    def rmsnorm_T(gname_l, gname, tagread):
        for j in range(NJ):
            kb.op("act", lambda e: e.activation(out=XN[j % 2][:], in_=X[:, j, :], func=AF.Square, accum_out=SS[:, j:j + 1]),
                  reads=["X%d" % j], writes=["xn%d" % (j % 2), "ss%d" % j])
        kb.op("act", lambda e: e.activation(out=SS[:, 0:NJ], in_=SS[:, 0:NJ], func=AF.Ln, scale=1.0 / D, bias=1e-6),
              reads=["ss%d" % j for j in range(NJ)], writes=["ss%d" % j for j in range(NJ)])
        kb.op("act", lambda e: e.activation(out=SS[:, 0:NJ], in_=SS[:, 0:NJ], func=AF.Exp, scale=-0.5),
              reads=["ss%d" % j for j in range(NJ)], writes=["ss%d" % j for j in range(NJ)])
        for j in range(NJ):
            xn = XN[j % 2]
            kb.op("dve", lambda e: e.tensor_scalar(out=xn[:], in0=X[:, j, :], scalar1=SS[:, j:j + 1], scalar2=None,
                                                   op0=ALU.mult), reads=["X%d" % j, "ss%d" % j], writes=["xn%d" % (j % 2)])
            pstE = PS[7][:].bitcast(BF16)
            pstO = PS[6][:].bitcast(BF16)
            for c in range(8):
                pst_c = (pstO if c % 2 else pstE)[:, (c // 2) * 128:(c // 2 + 1) * 128]
                kb.op("pe", lambda e: e.transpose(out=pst_c, in_=xn[:, c * 128:(c + 1) * 128],
                                                  identity=IDB[:]), reads=["xn%d" % (j % 2), "idb"], writes=["ps6" if c % 2 else "ps7"], acc=True)
            for c in range(8):
                pst_c = (pstO if c % 2 else pstE)[:, (c // 2) * 128:(c // 2 + 1) * 128]
                kb.op("act" if c % 2 else "dve",
                      (lambda e: e.activation(out=HT[:, c, j * 128:(j + 1) * 128], in_=pst_c,
                                              func=AF.Copy, scale=pv(gname_l, gname, c))) if c % 2 else
                      (lambda e: e.tensor_scalar(out=HT[:, c, j * 128:(j + 1) * 128], in0=pst_c,
                                                 scalar1=pv(gname_l, gname, c), scalar2=None, op0=ALU.mult)),
                      reads=["ps6" if c % 2 else "ps7", "pv"], writes=["HT%d" % c])

    def proj_fm(l, wv, wkey, ci, ps, pskey, nk=8, rhs=None, rkey="HT"):
        rhs = HT if rhs is None else rhs
        for k in range(nk):
            kb.op("pe", lambda e: e.matmul(ps, lhsT=wv[:, k, ci * 128:(ci + 1) * 128], rhs=rhs[:, k, :],
                                           start=(k == 0), stop=(k == nk - 1)),
                  reads=[wkey, (rkey + str(k)) if rkey == "HT" else rkey], writes=[pskey], acc=True)

    def proj_tm(l, wv, wkey, j, ps, pskey, ncol, nk=8, lhs=None, lkey="HT"):
        lhs = HT if lhs is None else lhs
        for k in range(nk):
            kb.op("pe", lambda e: e.matmul(ps, lhsT=lhs[:, k, j * 128:(j + 1) * 128], rhs=wv[:, k, 0:ncol],
                                           start=(k == 0), stop=(k == nk - 1)),
                  reads=[wkey, (lkey + str(k)) if lkey == "HT" else lkey], writes=[pskey], acc=True)

    def bank(i, w=None):
        w = TT if w is None else w
        return PS[i][:, 0:w]

    def pool_stage(l, it):
        arena_reset()
        XA = ab("xa", [128, 4, 16 + TT])
        PD = ab("pd", [128, TT], BF16)
        PSUMS = [ab("psum_s%d" % i, [128, 16 + TT]) for i in range(2)]
        wv, wk = load_w("in", l, wview("in", l, 0, 8, C_A, 512), 8, 512)
        wins = (2, 4, 8, 16)
        for g in range(4):
            pb = g % 2
            proj_fm(l, wv, wk, g, bank(pb), "ps%d" % pb)
            kb.op("pool", lambda e: e.tensor_copy(out=XA[:, g, 0:16], in_=POOLH[:, l, g, :]), reads=["poolh"], writes=["xa%d" % g])
            kb.op("act", lambda e: e.copy(out=XA[:, g, 16:16 + TT], in_=bank(pb)), reads=["ps%d" % pb], writes=["xa%d" % g])
            kb.op("pool", lambda e: e.tensor_copy(out=POOLH[:, l, g, :], in_=XA[:, g, TT:TT + 16]), reads=["xa%d" % g], writes=["poolh"])
            cur = XA[:, g, :]
            ckey = "xa%d" % g
            sh = 1
            bi = 0
            while sh < wins[g]:
                nxt = PSUMS[bi]
                nkey = "psums%d" % bi
                lo = 2 * sh - 1
                kb.op("dve", lambda e: e.tensor_tensor(out=nxt[:, lo:16 + TT], in0=cur[:, lo:16 + TT],
                                                       in1=cur[:, lo - sh:16 + TT - sh], op=ALU.add),
                      reads=[ckey], writes=[nkey])
                cur, ckey = nxt, nkey
                sh *= 2
                bi ^= 1
            dst = PSUMS[bi]
            dkey = "psums%d" % bi
            if it == 0:
                kb.op("dve", lambda e: e.tensor_tensor(out=cur[:, 16:32], in0=cur[:, 16:32],
                                                       in1=CST[:, 1024 + g * 16:1024 + (g + 1) * 16], op=ALU.mult),
                      reads=[ckey, "cst"], writes=[ckey])
            kb.op("dve", lambda e: e.scalar_tensor_tensor(out=PD[:], in0=cur[:, 16:16 + TT], scalar=1.0 / wins[g],
                                                          in1=XA[:, g, 16:16 + TT], op0=ALU.mult, op1=ALU.subtract),
                  reads=[ckey, "xa%d" % g], writes=["pd"])
            kb.op("pe", lambda e: e.matmul(bank(2 + pb), lhsT=POOLW[:, l, g * 128:(g + 1) * 128], rhs=PD[:], start=True, stop=True),
                  reads=["poolw", "pd"], writes=["ps%d" % (2 + pb)])
            kb.op("act", lambda e: e.activation(out=BR[0][:, g, :], in_=bank(2 + pb), func=AF.Copy, scale=pv(l, "pool_scale", g)),
                  reads=["ps%d" % (2 + pb), "pv"], writes=["br0"])

    def sg_stage(l, it):
        arena_reset()
        SGU = ab("sgu", [128, 4, TT], BF16)
        SGV = ab("sgv", [128, 512])
        SGVN = ab("sgvn", [128, 512], BF16)
        BNS = ab("bns", [128, 8])
        BNA = ab("bna", [128, 4])
        wv, wk = load_w("in", l, wview("in", l, 0, 8, C_C, 512), 8, 512)
        for c in range(4):
            pb = c % 2
            proj_fm(l, wv, wk, c, bank(pb), "ps%d" % pb)
            kb.op("act", lambda e: e.activation(out=SGU[:, c, :], in_=bank(pb), func=AF.Gelu_apprx_tanh),
                  reads=["ps%d" % pb], writes=["sgu"])
        wv, wk = load_w("in", l, wview("in", l, 0, 8, C_C + 512, 512), 8, 512)
        lnw = BC[:, l * BR_W:l * BR_W + 512]
        lnb = BC[:, l * BR_W + 512:l * BR_W + 1024]
        for j in range(NJ):
            pb = 2 + (j % 2)
            proj_tm(l, wv, wk, j, PS[pb][:, 0:512], "ps%d" % pb, 512)
            kb.op("act", lambda e: e.activation(out=SGV[:], in_=PS[pb][:, 0:512], func=AF.Gelu_apprx_tanh),
                  reads=["ps%d" % pb], writes=["sgv"])
            kb.op("dve", lambda e: e.bn_stats(out=BNS[:, 0:6], in_=SGV[:]), reads=["sgv"], writes=["bns"])
            kb.op("dve", lambda e: e.bn_aggr(out=BNA[:, 0:2], in_=BNS[:, 0:6]), reads=["bns"], writes=["bna"])
            kb.op("act", lambda e: e.activation(out=BNA[:, 2:3], in_=BNA[:, 1:2], func=AF.Ln, bias=1e-5), reads=["bna"], writes=["bna"])
            kb.op("act", lambda e: e.activation(out=BNA[:, 2:3], in_=BNA[:, 2:3], func=AF.Exp, scale=-0.5), reads=["bna"], writes=["bna"])
            kb.op("dve", lambda e: e.tensor_scalar(out=SGV[:], in0=SGV[:], scalar1=BNA[:, 0:1], scalar2=BNA[:, 2:3],
                                                   op0=ALU.subtract, op1=ALU.mult), reads=["sgv", "bna"], writes=["sgv"])
            kb.op("pool", lambda e: e.tensor_tensor(out=SGV[:], in0=SGV[:], in1=lnw, op=ALU.mult), reads=["sgv", "bc"], writes=["sgv"])
            kb.op("pool", lambda e: e.tensor_tensor(out=SGVN[:], in0=SGV[:], in1=lnb, op=ALU.add), reads=["sgv", "bc"], writes=["sgvn"])
            for g in range(4):
                pg = 4 + g
                o = PS[pg][:, j * 128:(j + 1) * 128]
                kb.op("pe", lambda e: e.matmul(o, lhsT=SGVN[:, g * 128:(g + 1) * 128], rhs=SGWT[:, l, g * 128:(g + 1) * 128],
                                               start=True, stop=False), reads=["sgvn", "sgwt"], writes=["ps%d" % pg], acc=True)
                kb.op("pe", lambda e: e.matmul(o, lhsT=ONEB[0:1, :], rhs=SGB[0:1, l, g * 128:(g + 1) * 128],
                                               start=False, stop=True), reads=["oneb", "sgb"], writes=["ps%d" % pg], acc=True)
        for g in range(4):
            kb.op("dve", lambda e: e.tensor_tensor(out=BR[2][:, g, :], in0=bank(4 + g), in1=SGU[:, g, :], op=ALU.mult),
                  reads=["ps%d" % (4 + g), "sgu"], writes=["br2"])

    def zero_branch(k):
        kb.op("pool", lambda e: e.memset(BR[k][:], 0.0), writes=["br%d" % k])

    def merge_stage(l, it):
        arena_reset()
        ACC = ab("ACC", [128, 8, TT])
        MT = ab("MT", [128, 8, TT], BF16)
        GS = [ab("GS%d" % i, [128, TT]) for i in range(2)]
        for k in range(4):
            for half in range(2):
                wg_v, wg_k = load_w("in", l, wview("in", l, 0, 8, C_G + k * 1024 + half * 512, 512), 8, 512, slot=(half, None))
                wb_v, wb_k = load_w("br", l, wview("br", l, k * 512, 4, half * 512, 512), 4, 512, slot=(2, half))
                for dq in range(4):
                    dc = half * 4 + dq
                    pa, pb = dq % 2, 2 + dq % 2
                    proj_fm(l, wg_v, wg_k, dq, bank(pa), "ps%d" % pa)
                    gs = GS[dq % 2]
                    kb.op("act", lambda e: e.activation(out=gs[:], in_=bank(pa), func=AF.Sigmoid),
                          reads=["ps%d" % pa], writes=["gs%d" % (dq % 2)])
                    proj_fm(l, wb_v, wb_k, dq, bank(pb), "ps%d" % pb, nk=4, rhs=BR[k], rkey="br%d" % k)
                    if k == 0:
                        kb.op("dve", lambda e: e.tensor_tensor(out=ACC[:, dc, :], in0=bank(pb), in1=gs[:], op=ALU.mult),
                              reads=["ps%d" % pb, "gs%d" % (dq % 2)], writes=["acc%d" % dc])
                    else:
                        kb.op("dve", lambda e: e.tensor_tensor(out=gs[:], in0=bank(pb), in1=gs[:], op=ALU.mult),
                              reads=["ps%d" % pb, "gs%d" % (dq % 2)], writes=["gs%d" % (dq % 2)])
                        if k < 3:
                            kb.op("pool", lambda e: e.tensor_tensor(out=ACC[:, dc, :], in0=ACC[:, dc, :], in1=gs[:], op=ALU.add),
                                  reads=["acc%d" % dc, "gs%d" % (dq % 2)], writes=["acc%d" % dc])
                        else:
                            kb.op("pool", lambda e: e.tensor_tensor(out=MT[:, dc, :], in0=ACC[:, dc, :], in1=gs[:], op=ALU.add),
                                  reads=["acc%d" % dc, "gs%d" % (dq % 2)], writes=["mt"])
        add_tm("out", l, None, 8, MT, "mt")

    def add_tm(name, l, s, nk, lhs, lkey):
        for half in range(2):
            wv, wk = load_w(name, l, wview(name, l, 0, nk, half * 512, 512), nk, 512)
            for j in range(NJ):
                pb = (half * NJ + j) % 4
                proj_tm(l, wv, wk, j, PS[pb][:, 0:512], "ps%d" % pb, 512, nk=nk, lhs=lhs, lkey=lkey)
                kb.op("dve", lambda e: e.tensor_tensor(out=X[:, j, half * 512:(half + 1) * 512], in0=PS[pb][:, 0:512],
                                                       in1=X[:, j, half * 512:(half + 1) * 512], op=ALU.add),
                      reads=["ps%d" % pb, "X%d" % j], writes=["X%d" % j])


    CARRYU = sb("carryu", [128, DEPTH, 44, 2])
    kb.op("pool", lambda e: e.memset(CARRYU[:], 0.0), writes=["carryu"])

    def ffn_stage(l, it):
        arena_reset()
        ACTT = ab("actt", [128, 22, TT], BF16)
        GG = ab("gg", [128, TT])
        RAWU = [ab("rawu%d" % i, [128, 2 + TT]) for i in range(2)]
        CV = [ab("cv%d" % i, [128, TT]) for i in range(2)]
        rmsnorm_T(l, "g_ffn", None)
        n = 0
        for gq in range(6):
            nch = min(4, 22 - 4 * gq)
            wg = load_w("up", l, wview("up", l, 0, 8, gq * 512, nch * 128), 8, nch * 128)
            wvv = load_w("up", l, wview("up", l, 0, 8, DFF + gq * 512, nch * 128), 8, nch * 128)
            for q in range(nch):
                p = 4 * gq + q
                for which, (wv, wk), ci in ((0, wg, p), (1, wvv, 22 + p)):
                    pb = n % 4
                    n += 1
                    proj_fm(l, wv, wk, q, bank(pb), "ps%d" % pb)
                    raw = RAWU[which]
                    rk = "rawu%d" % which
                    kb.op("pool", lambda e: e.tensor_copy(out=raw[:, 0:2], in_=CARRYU[:, l, ci, :]), reads=["carryu"], writes=[rk])
                    kb.op("act", lambda e: e.copy(out=raw[:, 2:2 + TT], in_=bank(pb)), reads=["ps%d" % pb], writes=[rk])
                    kb.op("pool", lambda e: e.tensor_copy(out=CARRYU[:, l, ci, :], in_=raw[:, TT:TT + 2]), reads=[rk], writes=["carryu"])
                    cv = CV[which]
                    ck = "cv%d" % which
                    kb.op("act", lambda e: e.activation(out=cv[:], in_=bank(pb), func=AF.Identity, scale=pv(l, "cw2", ci),
                                                        bias=pv(l, "cb", ci)), reads=["ps%d" % pb, "pv"], writes=[ck])
                    kb.op("dve", lambda e: e.scalar_tensor_tensor(out=cv[:], in0=raw[:, 1:1 + TT], scalar=pv(l, "cw1", ci),
                                                                  in1=cv[:], op0=ALU.mult, op1=ALU.add),
                          reads=[rk, "pv", ck], writes=[ck])
                    kb.op("dve", lambda e: e.scalar_tensor_tensor(out=cv[:], in0=raw[:, 0:TT], scalar=pv(l, "cw0", ci),
                                                                 in1=cv[:], op0=ALU.mult, op1=ALU.add),
                          reads=[rk, "pv", ck], writes=[ck])
                kb.op("act", lambda e: e.activation(out=GG[:], in_=CV[0][:], func=AF.Gelu_apprx_tanh), reads=["cv0"], writes=["gg"])
                kb.op("dve", lambda e: e.tensor_tensor(out=ACTT[:, p, :], in0=GG[:], in1=CV[1][:], op=ALU.mult),
                      reads=["gg", "cv1"], writes=["actt"])
        for half in range(2):
            for kg in range(3):
                nk = 8 if kg < 2 else 6
                wv, wk = load_w("dn", l, wview("dn", l, kg * 1024, nk, half * 512, 512), nk, 512)
                for j in range(NJ):
                    pb = 4 + (half * NJ + j) % 4
                    for k in range(nk):
                        kk_ = kg * 8 + k
                        kb.op("pe", lambda e: e.matmul(PS[pb][:, 0:512], lhsT=ACTT[:, kk_, j * 128:(j + 1) * 128], rhs=wv[:, k, :],
                                                       start=(kk_ == 0), stop=(kk_ == 21)),
                              reads=[wk, "actt"], writes=["ps%d" % pb], acc=True)
            for j in range(NJ):
                pb = 4 + (half * NJ + j) % 4
                kb.op("dve", lambda e: e.tensor_tensor(out=X[:, j, half * 512:(half + 1) * 512], in0=PS[pb][:, 0:512],
                                                       in1=X[:, j, half * 512:(half + 1) * 512], op=ALU.add),
                      reads=["ps%d" % pb, "X%d" % j], writes=["X%d" % j])


    def ple_stage(l, it):
        PTM = ab("ptm", [128, NJ, PLE])
        PTB = ab("ptb", [128, NJ, PLE], BF16)
        PT = ab("pt", [128, 2, TT], BF16)
        rmsnorm_T(l, "g_ple", None)
        for j in range(NJ):
            kb.dma("sp", PTM[:, j, :], p_d[l, it * TT + j * 128: it * TT + (j + 1) * 128, :], writes=["ptm"])
        kb.op("pool", lambda e: e.tensor_copy(out=PTB[:], in_=PTM[:]), reads=["ptm"], writes=["ptb"])
        pst = PS[7][:].bitcast(BF16)
        for j in range(NJ):
            for c in range(2):
                kb.op("pe", lambda e: e.transpose(out=pst[:, (j * 2 + c) * 128:(j * 2 + c + 1) * 128], in_=PTB[:, j, c * 128:(c + 1) * 128],
                                                  identity=IDB[:]), reads=["ptb", "idb"], writes=["ps7"], acc=True)
        for j in range(NJ):
            for c in range(2):
                kb.op("act", lambda e: e.copy(out=PT[:, c, j * 128:(j + 1) * 128], in_=pst[:, (j * 2 + c) * 128:(j * 2 + c + 1) * 128]),
                      reads=["ps7"], writes=["pt"])
        for half in range(2):
            wg = load_w("pg", l, wview("pg", l, 0, 8, half * 512, 512), 8, 512)
            wp = load_w("pp", l, wview("pp", l, 0, 2, half * 512, 512), 2, 512)
            for j in range(NJ):
                pa, pb = j % 2, 2 + j % 2
                proj_tm(l, wg[0], wg[1], j, PS[pa][:, 0:512], "ps%d" % pa, 512)
                tmp = TMP[j % 2]
                kb.op("act", lambda e: e.activation(out=tmp[:], in_=PS[pa][:, 0:512], func=AF.Sigmoid),
                      reads=["ps%d" % pa], writes=["tmp%d" % (j % 2)])
                proj_tm(l, wp[0], wp[1], j, PS[pb][:, 0:512], "ps%d" % pb, 512, nk=2, lhs=PT, lkey="pt")
                kb.op("dve", lambda e: e.tensor_tensor(out=tmp[:], in0=PS[pb][:, 0:512], in1=tmp[:], op=ALU.mult),
                      reads=["ps%d" % pb, "tmp%d" % (j % 2)], writes=["tmp%d" % (j % 2)])
                kb.op("pool", lambda e: e.tensor_tensor(out=X[:, j, half * 512:(half + 1) * 512], in0=tmp[:],
                                                        in1=X[:, j, half * 512:(half + 1) * 512], op=ALU.add),
                      reads=["tmp%d" % (j % 2), "X%d" % j], writes=["X%d" % j])

    OMLB = sb("omlb", [128, DEPTH, 512])
    DIAG = sb("diag", [128, 128])
    for l in range(nlayers):
        for h in range(4):
            kb.op("dve", lambda e: e.tensor_scalar(out=DIAG[:], in0=IDENT, scalar1=OML[:, l, h:h + 1], scalar2=None, op0=ALU.mult),
                  reads=["cst", "oml"], writes=["diag"])
            kb.op("pe", lambda e: e.matmul(PS[0][:, 0:128], lhsT=ONES, rhs=DIAG[:], start=True, stop=True),
                  reads=["cst", "diag"], writes=["ps0"])
            kb.op("act", lambda e: e.copy(out=OMLB[:, l, h * 128:(h + 1) * 128], in_=PS[0][:, 0:128]), reads=["ps0"], writes=["omlb"])
    HS = sb("hs", [128, DEPTH, 4, 128])
    HSB = sb("hsb", [128, DEPTH, 4, 128], BF16)
    kb.op("pool", lambda e: e.memset(HS[:], 0.0), writes=["hs0", "hs1", "hs2", "hs3"])
    kb.op("pool", lambda e: e.memset(HSB[:], 0.0), writes=["hsb0", "hsb1", "hsb2", "hsb3"])

    def hgrn_stage(l, it):
        arena_reset()
        HQ = ab("hq", [128, 4, TT])
        HKEY = ab("hkey", [128, 4, TT])
        HLG = ab("hlg", [128, NJ, 512])
        HKTM = ab("hktm", [128, NJ, 512])
        HV = ab("hv", [128, NJ, 512], BF16)
        HQT = ab("hqt", [128, 4, TT], BF16)
        HKT = ab("hkt", [128, 4, 128], BF16)
        HE = ab("he", [128, 4, 128])
        HE2 = ab("he2", [128, 4, 128])
        HDC = ab("hdc", [128, 4, 4])
        HKH = ab("hkh", [128, 512], BF16)
        HKH3 = ab("hkh3", [128, 512], BF16)
        HSC = ab("hsc", [128, 4, 128], BF16)
        HSQ = ab("hsq", [128, TT])
        HR = ab("hr", [128, TT])
        wq = load_w("in", l, wview("in", l, 0, 8, C_D, 512), 8, 512)
        wf = load_w("in", l, wview("in", l, 0, 8, C_D + 512, 512), 8, 512)
        wi = load_w("in", l, wview("in", l, 0, 8, C_D + 1024, 512), 8, 512)
        for h in range(4):
            pb = h % 2
            proj_fm(l, wq[0], wq[1], h, bank(pb), "ps%d" % pb)
            kb.op("act", lambda e: e.activation(out=HQ[:, h, :], in_=bank(pb), func=AF.Silu), reads=["ps%d" % pb], writes=["hq"])
            pb = 2 + h % 2
            proj_fm(l, wf[0], wf[1], h, bank(pb), "ps%d" % pb)
            kb.op("act", lambda e: e.activation(out=HKEY[:, h, :], in_=bank(pb), func=AF.Sigmoid, scale=-1.0),
                  reads=["ps%d" % pb], writes=["hkey%d" % h])
            kb.op("dve", lambda e: e.tensor_scalar(out=HKEY[:, h, :], in0=HKEY[:, h, :], scalar1=OML[:, l, h:h + 1], scalar2=None,
                                                   op0=ALU.mult), reads=["hkey%d" % h, "oml"], writes=["hkey%d" % h])
        for j in range(NJ):
            pb = j % 2
            proj_tm(l, wf[0], wf[1], j, PS[pb][:, 0:512], "ps%d" % pb, 512)
            kb.op("act", lambda e: e.activation(out=HKTM[:, j, :], in_=PS[pb][:, 0:512], func=AF.Sigmoid, scale=-1.0),
                  reads=["ps%d" % pb], writes=["hktm%d" % j])
            kb.op("dve", lambda e: e.tensor_tensor(out=HKTM[:, j, :], in0=HKTM[:, j, :], in1=OMLB[:, l, :], op=ALU.mult),
                  reads=["hktm%d" % j, "omlb"], writes=["hktm%d" % j])
            kb.op("dve", lambda e: e.tensor_scalar(out=HLG[:, j, :], in0=HKTM[:, j, :], scalar1=-1.0, scalar2=1.0,
                                                   op0=ALU.mult, op1=ALU.add), reads=["hktm%d" % j], writes=["hlg%d" % j])
            kb.op("dve", lambda e: e.tensor_scalar(out=HLG[:, j, :], in0=HLG[:, j, :], scalar1=1e-30, scalar2=None, op0=ALU.max),
                  reads=["hlg%d" % j], writes=["hlg%d" % j])
            kb.op("act", lambda e: e.activation(out=HLG[:, j, :], in_=HLG[:, j, :], func=AF.Ln), reads=["hlg%d" % j], writes=["hlg%d" % j])
            pb = 2 + j % 2
            proj_tm(l, wi[0], wi[1], j, PS[pb][:, 0:512], "ps%d" % pb, 512)
            kb.op("act", lambda e: e.copy(out=HV[:, j, :], in_=PS[pb][:, 0:512]), reads=["ps%d" % pb], writes=["hv%d" % j])
        for j in range(NJ):
            js = slice(j * 128, (j + 1) * 128)
            kb.op("pe", lambda e: e.matmul(PS[1][:, 0:512], lhsT=MH_GT, rhs=HLG[:, j, :], start=True, stop=True),
                  reads=["cst", "hlg%d" % j], writes=["ps1"])
            kb.op("act", lambda e: e.activation(out=TMP[0][:], in_=PS[1][:, 0:512], func=AF.Exp), reads=["ps1"], writes=["tmp0"])
            kb.op("dve", lambda e: e.tensor_tensor(out=HKH[:], in0=TMP[0][:], in1=HKTM[:, j, :], op=ALU.mult),
                  reads=["tmp0", "hktm%d" % j], writes=["hkh"])
            kb.op("dve", lambda e: e.tensor_scalar(out=HKH3[64:128, :], in0=HKH[64:128, :], scalar1=CST[64:128, 384 + 95:384 + 96],
                                                    scalar2=None, op0=ALU.mult), reads=["hkh", "cst"], writes=["hkh3"])
            for h in range(4):
                hs = slice(h * 128, (h + 1) * 128)
                kb.op("pe", lambda e: e.matmul(PS[0][:, hs], lhsT=HLG[:, j, hs], rhs=MH_LE, start=True, stop=True),
                      reads=["cst", "hlg%d" % j], writes=["ps0"], acc=True)
            kb.op("act", lambda e: e.activation(out=HE[:].rearrange("p h t -> p (h t)"), in_=PS[0][:, 0:512], func=AF.Exp),
                  reads=["ps0"], writes=["he"])
            kb.op("act", lambda e: e.activation(out=HE2[:].rearrange("p h t -> p (h t)"), in_=PS[0][:, 0:512], func=AF.Exp, scale=-1.0),
                  reads=["ps0"], writes=["he2"])
            kb.op("dve", lambda e: e.tensor_tensor(out=HQT[:, :, js], in0=HQ[:, :, js], in1=HE[:], op=ALU.mult),
                  reads=["hq", "he"], writes=["hqt"])
            kb.op("dve", lambda e: e.tensor_tensor(out=HKT[:], in0=HKEY[:, :, js], in1=HE2[:], op=ALU.mult),
                  reads=["hkey0", "hkey1", "hkey2", "hkey3", "he2"], writes=["hkt"])
            for h in range(4):
                hs = slice(h * 128, (h + 1) * 128)
                kb.op("pe", lambda e: e.matmul(PS[2][:, hs], lhsT=HKT[:, h, :], rhs=HQT[:, h, js], start=True, stop=True),
                      reads=["hkt", "hqt"], writes=["ps2"], acc=True)
            kb.op("dve", lambda e: e.tensor_tensor(out=HSC[:], in0=PS[2][:, 0:512].rearrange("p (h t) -> p h t", h=4),
                                                   in1=MH_LE.unsqueeze(1).to_broadcast([128, 4, 128]), op=ALU.mult),
                  reads=["ps2", "cst"], writes=["hsc"])
            for h in range(4):
                hs = slice(h * 128, (h + 1) * 128)
                kb.op("pe", lambda e: e.matmul(PS[4 + h][:, js], lhsT=HV[:, j, hs], rhs=HSC[:, h, :], start=True, stop=False),
                      reads=["hv%d" % j, "hsc"], writes=["ps%d" % (4 + h)])
            for c in range(4):
                cs = slice(32 * c, 32 * c + 32)
                for h in range(4):
                    hs = slice(h * 128, (h + 1) * 128)
                    kb.op("pe", lambda e: e.matmul(PS[4 + h][:, j * 128 + 32 * c:j * 128 + 32 * c + 32], lhsT=HSB[:, l, h, :],
                                                   rhs=HQT[:, h, j * 128 + 32 * c:j * 128 + 32 * c + 32], start=False, stop=(c == 3),
                                                   skip_group_check=(c < 3)),
                          reads=["hsb%d" % h, "hqt"], writes=["ps%d" % (4 + h)], acc=True)
                    if c < 3:
                        kb.op("pe", lambda e: e.matmul(PS[h][:, 0:128], lhsT=HKH[cs, hs], rhs=HV[cs, j, hs], start=True, stop=True),
                              reads=["hkh", "hv%d" % j], writes=["ps%d" % h])
                    else:
                        kb.op("pe", lambda e: e.matmul(PS[h][:, 0:128], lhsT=HKH3[64:128, hs], rhs=HV[64:128, j, hs], start=True, stop=True),
                              reads=["hkh3", "hv%d" % j], writes=["ps%d" % h])
                    kb.op("dve", lambda e: e.scalar_tensor_tensor(out=HS[:, l, h, :], in0=HS[:, l, h, :], scalar=HE[:, h, 32 * c + 31:32 * c + 32],
                                                                  in1=PS[h][:, 0:128], op0=ALU.mult, op1=ALU.add),
                          reads=["hs%d" % h, "he", "ps%d" % h], writes=["hs%d" % h])
                    kb.op("act", lambda e: e.copy(out=HSB[:, l, h, :], in_=HS[:, l, h, :]), reads=["hs%d" % h], writes=["hsb%d" % h])
        for h in range(4):
            pb = h % 2
            kb.op("act", lambda e: e.activation(out=HSQ[:].bitcast(BF16)[:, 0:TT], in_=bank(4 + h), func=AF.Square), reads=["ps%d" % (4 + h)], writes=["hsq"])
            kb.op("pe", lambda e: e.matmul(bank(pb), lhsT=ONEB[:], rhs=HSQ[:].bitcast(BF16)[:, 0:TT], start=True, stop=True), reads=["oneb", "hsq"], writes=["ps%d" % pb])
            kb.op("act", lambda e: e.activation(out=HR[:], in_=bank(pb), func=AF.Ln, scale=1.0 / 128, bias=1e-6), reads=["ps%d" % pb], writes=["hr"])
            kb.op("act", lambda e: e.activation(out=HR[:], in_=HR[:], func=AF.Exp, scale=-0.5), reads=["hr"], writes=["hr"])
            kb.op("dve", lambda e: e.scalar_tensor_tensor(out=BR[3][:, h, :], in0=bank(4 + h), scalar=pv(l, "hnorm", h), in1=HR[:],
                                                          op0=ALU.mult, op1=ALU.mult),
                  reads=["ps%d" % (4 + h), "pv", "hr"], writes=["br3"])


    RCARRY = sb("rcarry", [128, DEPTH, 14])
    kb.op("pool", lambda e: e.memset(RCARRY[:], 0.0), writes=["rcarry"])
    RS = sb("rs", [128, DEPTH, 4, 64])
    RSB = sb("rsb", [128, DEPTH, 4, 64], BF16)
    kb.op("pool", lambda e: e.memset(RS[:], 0.0), writes=["rs0", "rs1", "rs2", "rs3"])
    kb.op("pool", lambda e: e.memset(RSB[:], 0.0), writes=["rsb0", "rsb1", "rsb2", "rsb3"])
    NQ = 2 * NJ
    ARr = sb("ar_r", [128, NJ, 2, 128], RDT)
    BTr = sb("bt_r", [128, TT], RDT)
    BT = BTr
    KTr = sb("kt_r", [128, TT], RDT)
    KT = KTr
    BTMr = sb("btm_r", [128, NJ, 128], RDT)
    BTM = BTMr
    KTMr = sb("ktm_r", [128, NJ, 128], RDT)
    KTM = KTMr
    VTMr = sb("vtm_r", [128, NJ, 128], RDT)
    VTM = VTMr
    RPr = [[sb("rp%d_%d_r" % (q, i), [128, 128], RDT) for i in range(2)] for q in range(NQ)]
    RP = RPr
    QT = [sb("qt%d" % q, [128, 7, 128], RDT) for q in range(NQ)]
    RZ2 = sb("rz2", [128, 128], RDT)
    RU2 = sb("ru2", [128, 128], RDT)
    prr = [0]
    evr = [0]

    def rbank():
        i = prr[0]
        prr[0] = (i + 1) % 5
        return i

    def rwkv_stage(l, it):
        arena_reset()
        RD = ab("rd", [128, TT])
        XL = ab("xl", [128, 14, TT])
        TWLA = ab("twla", [128, TT], BF16)
        SGL = ab("sgl", [128, TT], BF16)
        SIGD = ab("sigd", [128, TT])
        A_ = ab("a_", [128, TT])
        GATE = ab("gate", [128, TT])
        KKt = ab("kkt", [128, TT])
        SQ = ab("sq", [128, TT])
        RN = ab("rn", [128, TT])
        KM = ab("km", [128, TT])
        BON = ab("bon", [128, TT])
        CS = ab("cs", [128, TT])
        EIN = ab("ein", [128, TT])
        ENEG = ab("eneg", [128, TT])
        EEX = ab("eex", [128, TT])
        DCOL = ab("dcol", [128, NJ])
        RTT = ab("rtt", [128, 64])
        YS = ab("ys", [128, TT])
        YC = ab("yc", [128, TT])
        RRAW = [ab("rraw%d" % i, [128, 1 + TT]) for i in range(2)]
        for grp in range(4):
            nch = 4 if grp < 3 else 2
            wv, wk = load_w("in", l, wview("in", l, 0, 8, C_B + grp * 512, nch * 128), 8, nch * 128)
            for q in range(nch):
                ci = grp * 4 + q
                pb = rbank()
                proj_fm(l, wv, wk, q, bank(pb), "ps%d" % pb)
                raw = RRAW[ci % 2]
                rk = "rraw%d" % (ci % 2)
                kb.op("pool", lambda e: e.tensor_copy(out=raw[:, 0:1], in_=RCARRY[:, l, ci:ci + 1]), reads=["rcarry"], writes=[rk])
                kb.op("act", lambda e: e.copy(out=raw[:, 1:1 + TT], in_=bank(pb)), reads=["ps%d" % pb], writes=[rk])
                kb.op("pool", lambda e: e.tensor_copy(out=RCARRY[:, l, ci:ci + 1], in_=raw[:, TT:TT + 1]), reads=[rk], writes=["rcarry"])
                kb.op("dve", lambda e: e.tensor_tensor(out=RD[:], in0=raw[:, 0:TT], in1=raw[:, 1:1 + TT], op=ALU.subtract),
                      reads=[rk], writes=["rd"])
                kb.op("dve", lambda e: e.scalar_tensor_tensor(out=XL[:, ci, :], in0=RD[:], scalar=pv(l, "mu", ci), in1=raw[:, 1:1 + TT],
                                                              op0=ALU.mult, op1=ALU.add), reads=["rd", rk, "pv"], writes=["xl%d" % ci])
        kb.op("act", lambda e: e.activation(out=TWLA[0:64, :], in_=XL[0:64, 12, :], func=AF.Tanh), reads=["xl12"], writes=["twla"])
        kb.op("pool", lambda e: e.tensor_copy(out=TWLA[64:128, :], in_=XL[64:128, 12, :]), reads=["xl12"], writes=["twla"])
        kb.op("act", lambda e: e.activation(out=SGL[:], in_=XL[:, 13, :], func=AF.Sigmoid), reads=["xl13"], writes=["sgl"])
        for c in range(4):
            cs_ = slice(c * 128, (c + 1) * 128)
            rkey, kkey, vkey = "xl%d" % c, "xl%d" % (4 + c), "xl%d" % (8 + c)
            r_, k_, v_ = XL[:, c, :], XL[:, 4 + c, :], XL[:, 8 + c, :]
            pb = rbank()
            kb.op("pe", lambda e: e.matmul(bank(pb), lhsT=W2A2[0:64, l, cs_], rhs=TWLA[0:64, :], start=True, stop=True),
                  reads=["w2a2", "twla"], writes=["ps%d" % pb])
            kb.op("act", lambda e: e.activation(out=SIGD[:], in_=bank(pb), func=AF.Sigmoid, bias=pv(l, "w0", c)),
                  reads=["ps%d" % pb, "pv"], writes=["sigd"])
            pb = rbank()
            kb.op("pe", lambda e: e.matmul(bank(pb), lhsT=W2A2[64:128, l, cs_], rhs=TWLA[64:128, :], start=True, stop=True),
                  reads=["w2a2", "twla"], writes=["ps%d" % pb])
            kb.op("act", lambda e: e.activation(out=A_[:], in_=bank(pb), func=AF.Sigmoid, bias=pv(l, "a0", c)),
                  reads=["ps%d" % pb, "pv"], writes=["a_"])
            pb = rbank()
            kb.op("pe", lambda e: e.matmul(bank(pb), lhsT=G2[:, l, cs_], rhs=SGL[:], start=True, stop=True),
                  reads=["g2", "sgl"], writes=["ps%d" % pb])
            kb.op("act", lambda e: e.copy(out=GATE[:], in_=bank(pb)), reads=["ps%d" % pb], writes=["gate"])
            kb.op("dve", lambda e: e.tensor_scalar(out=KKt[:], in0=k_, scalar1=pv(l, "kk", c), scalar2=None, op0=ALU.mult),
                  reads=[kkey, "pv"], writes=["kkt"])
            kb.op("act", lambda e: e.activation(out=SQ[:].bitcast(BF16)[:, 0:TT], in_=KKt[:], func=AF.Square), reads=["kkt"], writes=["sq"])
            pb = rbank()
            kb.op("pe", lambda e: e.matmul(bank(pb), lhsT=BLKB[:], rhs=SQ[:].bitcast(BF16)[:, 0:TT], start=True, stop=True), reads=["blkb", "sq"], writes=["ps%d" % pb])
            kb.op("act", lambda e: e.activation(out=RN[:], in_=bank(pb), func=AF.Ln, bias=1e-24), reads=["ps%d" % pb], writes=["rn"])
            kb.op("act", lambda e: e.activation(out=RN[:], in_=RN[:], func=AF.Exp, scale=-0.5), reads=["rn"], writes=["rn"])
            kb.op("dve", lambda e: e.tensor_tensor(out=KKt[:], in0=KKt[:], in1=RN[:], op=ALU.mult), reads=["kkt", "rn"], writes=["kkt"])
            kb.op("dve", lambda e: e.tensor_scalar(out=KM[:], in0=A_[:], scalar1=pv(l, "ka", c), scalar2=OMKA[:, l, c:c + 1],
                                                   op0=ALU.mult, op1=ALU.add), reads=["a_", "pv", "omka"], writes=["km"])
            kb.op("dve", lambda e: e.tensor_tensor(out=KM[:], in0=KM[:], in1=k_, op=ALU.mult), reads=["km", kkey], writes=["km"])
            kb.op("dve", lambda e: e.scalar_tensor_tensor(out=SQ[:].bitcast(BF16)[:, 0:TT], in0=r_, scalar=pv(l, "rk", c), in1=KM[:], op0=ALU.mult, op1=ALU.mult),
                  reads=[rkey, "pv", "km"], writes=["sq"])
            pb = rbank()
            kb.op("pe", lambda e: e.matmul(bank(pb), lhsT=BLKB[:], rhs=SQ[:].bitcast(BF16)[:, 0:TT], start=True, stop=True), reads=["blkb", "sq"], writes=["ps%d" % pb])
            kb.op("dve", lambda e: e.tensor_tensor(out=BON[:], in0=bank(pb), in1=v_, op=ALU.mult), reads=["ps%d" % pb, vkey], writes=["bon"])
            for j in range(NJ):
                js = slice(j * 128, (j + 1) * 128)
                kb.op("dve", lambda e: e.tensor_tensor_scan(out=CS[:, js], data0=ONES, data1=SIGD[:, js], initial=0.0,
                                                            op0=ALU.mult, op1=ALU.add), reads=["cst", "sigd"], writes=["cs"])
            kb.op("act", lambda e: e.activation(out=EIN[:], in_=CS[:], func=AF.Exp, scale=-LAMB), reads=["cs"], writes=["ein"])
            kb.op("act", lambda e: e.activation(out=ENEG[:], in_=CS[:], func=AF.Exp, scale=LAMB), reads=["cs"], writes=["eneg"])
            kb.op("dve", lambda e: e.tensor_tensor(out=EEX[:], in0=CS[:], in1=SIGD[:], op=ALU.subtract), reads=["cs", "sigd"], writes=["eex"])
            kb.op("act", lambda e: e.activation(out=EEX[:], in_=EEX[:], func=AF.Exp, scale=-LAMB), reads=["eex"], writes=["eex"])
            sub = lambda v: v.rearrange("p (j t) -> p j t", j=NJ)
            kb.op("dve", lambda e: e.scalar_tensor_tensor(out=ARr[:, :, 0, :], in0=sub(KKt[:]), scalar=-1.0, in1=sub(EEX[:]), op0=ALU.mult, op1=ALU.mult),
                  reads=["kkt", "eex"], writes=["at_"])
            kb.op("pool", lambda e: e.tensor_tensor(out=RN[:], in0=KKt[:], in1=A_[:], op=ALU.mult), reads=["kkt", "a_"], writes=["rn"])
            kb.op("dve", lambda e: e.tensor_tensor(out=BTr[:], in0=RN[:], in1=ENEG[:], op=ALU.mult), reads=["rn", "eneg"], writes=["bt"])
            kb.op("dve", lambda e: e.tensor_tensor(out=KTr[:], in0=KM[:], in1=ENEG[:], op=ALU.mult), reads=["km", "eneg"], writes=["kt"])
            kb.op("dve", lambda e: e.tensor_tensor(out=ARr[:, :, 1, :], in0=sub(r_), in1=sub(EIN[:]), op=ALU.mult), reads=[rkey, "ein"], writes=["rt"])
            for j in range(NJ):
                kb.op("pool", lambda e: e.tensor_copy(out=DCOL[:, j:j + 1], in_=EIN[:, j * 128 + 127:j * 128 + 128]), reads=["ein"], writes=["dcol"])
            for j in range(NJ):
                js = slice(j * 128, (j + 1) * 128)
                for src, skey, dst, dkey, isb in ((BT, "bt", BTMr, "btm", True), (KT, "kt", KTMr, "ktm", True),
                                                  (XL[:, 8 + c, :], vkey, VTMr, "vtm", False)):
                    pb = rbank()
                    sap = src[:, js]
                    po = PS[pb][:].bitcast(BF16)[:, 0:128] if isb else PS[pb][:, 0:128]
                    idn = IDB[:] if isb else IDENT
                    kb.op("pe", lambda e: e.transpose(out=po, in_=sap, identity=idn), reads=[skey, "cst", "idb"], writes=["ps%d" % pb])
                    kb.op("act", lambda e: e.copy(out=dst[:, j, :], in_=po), reads=["ps%d" % pb], writes=[dkey])
            combos = [(h, j) for j in range(NJ) for h in range(2)]

            def qi(h, j):
                return h * NJ + j

            def mm_ev(q, lhsT, rhs, lk, rk_, dst, dkey, mask=None, eng="dve", dst_in=None):
                pb = rbank()
                kb.op("pe", lambda e: e.matmul(PS[pb][:, 0:128], lhsT=lhsT, rhs=rhs, start=True, stop=True), reads=lk + rk_, writes=["ps%d" % pb])
                if mask is not None:
                    kb.op("dve", lambda e: e.tensor_tensor(out=dst[:], in0=PS[pb][:, 0:128], in1=mask, op=ALU.mult),
                          reads=["ps%d" % pb, "cst"], writes=[dkey])
                elif eng == "add":
                    kb.op("dve", lambda e: e.tensor_tensor(out=dst[:], in0=PS[pb][:, 0:128], in1=dst_in[:], op=ALU.add),
                          reads=["ps%d" % pb, dkey], writes=[dkey])
                else:
                    kb.op("act", lambda e: e.copy(out=dst[:], in_=PS[pb][:, 0:128]), reads=["ps%d" % pb], writes=[dkey])

            MASK2 = CST[:, 128:384].rearrange("p (a t) -> p a t", a=2)
            for (h, j) in combos:
                q = qi(h, j)
                hp = slice(64 * h, 64 * h + 64)
                js = slice(j * 128, (j + 1) * 128)
                for lhs_, lk_, o0, o1, dk in ((BTr, "bt", 0, 2, "qtA%d" % q), (KTr, "kt", 3, 4, "qtK%d" % q)):
                    pb = rbank()
                    kb.op("pe", lambda e: e.matmul(PS[pb][:, 0:256], lhsT=lhs_[hp, js], rhs=ARr[hp, j, :, :], start=True, stop=True),
                          reads=[lk_, "at_", "rt"], writes=["ps%d" % pb])
                    oap = QT[q][:, o0:o1 + 1:(o1 - o0), :]
                    kb.op("dve", lambda e: e.tensor_tensor(out=oap, in0=PS[pb][:, 0:256].rearrange("p (a t) -> p a t", a=2), in1=MASK2, op=ALU.mult),
                          reads=["ps%d" % pb, "cst"], writes=[dk])
                mm_ev(q, ARr[hp, j, 0, :], BTr[hp, js], ["at_"], ["bt"], RPr[q][0], "rp%d_0" % q, mask=M_GT)
                kb.op("pool", lambda e: e.tensor_tensor(out=QT[q][:, 6, :], in0=QT[q][:, 0, :], in1=IDENT, op=ALU.add),
                      reads=["qtA%d" % q, "cst"], writes=["qtR%d_1" % q])
            WS = (0, 5)
            for k in range(0, 7):
                x, y = k % 2, (k + 1) % 2
                for (h, j) in combos:
                    q = qi(h, j)
                    Wx, Wy = WS[x], WS[y]
                    ptkey = "qtA%d" % q if k == 0 else "qtP%d_%d" % (q, x)
                    if k <= 5:
                        mm_ev(q, QT[q][:, Wx, :], RPr[q][x][:], [ptkey], ["rp%d_%d" % (q, x)], RPr[q][y], "rp%d_%d" % (q, y))
                    if k == 0:
                        mm_ev(q, RPr[q][0][:], QT[q][:, Wx, :], ["rp%d_0" % q], [ptkey], QT[q][:, Wy, :], "qtP%d_%d" % (q, y))
                    elif k <= 4:
                        pb = rbank()
                        kb.op("pe", lambda e: e.matmul(PS[pb][:, 0:256], lhsT=RPr[q][x][:], rhs=QT[q][:, Wx:Wx + 2, :], start=True, stop=True),
                              reads=["rp%d_%d" % (q, x), ptkey, "qtR%d_%d" % (q, x)], writes=["ps%d" % pb])
                        kb.op("dve", lambda e: e.tensor_tensor(out=QT[q][:, Wy + 1, :], in0=PS[pb][:, 128:256], in1=QT[q][:, Wx + 1, :], op=ALU.add),
                              reads=["ps%d" % pb, "qtR%d_%d" % (q, x)], writes=["qtR%d_%d" % (q, y)])
                        kb.op("act", lambda e: e.copy(out=QT[q][:, Wy, :], in_=PS[pb][:, 0:128]),
                              reads=["ps%d" % pb, "qtR%d_%d" % (q, y)], writes=["qtP%d_%d" % (q, y)] + (["qtA%d" % q] if Wy == 0 else []))
                    else:
                        pb = rbank()
                        kb.op("pe", lambda e: e.matmul(PS[pb][:, 0:128], lhsT=RPr[q][x][:], rhs=QT[q][:, Wx + 1, :], start=True, stop=True),
                              reads=["rp%d_%d" % (q, x), "qtR%d_%d" % (q, x)], writes=["ps%d" % pb])
                        kb.op("dve", lambda e: e.tensor_tensor(out=QT[q][:, Wy + 1, :], in0=PS[pb][:, 0:128], in1=QT[q][:, Wx + 1, :], op=ALU.add),
                              reads=["ps%d" % pb, "qtR%d_%d" % (q, x)], writes=["qtR%d_%d" % (q, y)])
            RFIN = WS[7 % 2] + 1
            RFK = "qtR%%d_%d" % (7 % 2)
            for j in range(NJ):
                js = slice(j * 128, (j + 1) * 128)
                for h in range(2):
                    q = qi(h, j)
                    hp = slice(64 * h, 64 * h + 64)
                    zo = PS[7][:, h * 64:h * 64 + 64]
                    kb.op("pe", lambda e: e.matmul(zo, lhsT=ARr[hp, j, 0, :], rhs=RSB[hp, l, c, :], start=True, stop=False),
                          reads=["at_", "rsb%d" % c], writes=["ps7"], acc=(h == 1))
                    kb.op("pe", lambda e: e.matmul(zo, lhsT=QT[q][:, 3, :], rhs=VTMr[:, j, hp], start=False, stop=True),
                          reads=["qtK%d" % q, "vtm"], writes=["ps7"], acc=True)
                    if h == 1:
                        kb.op("act", lambda e: e.copy(out=RZ2[:], in_=PS[7][:, 0:128]), reads=["ps7"], writes=["rz"])
                for h in range(2):
                    q = qi(h, j)
                    uo = PS[7][:, 128 + h * 64:128 + h * 64 + 64]
                    kb.op("pe", lambda e: e.matmul(uo, lhsT=QT[q][:, RFIN, :], rhs=RZ2[:, h * 64:h * 64 + 64], start=True, stop=True),
                          reads=[RFK % q, "rz"], writes=["ps7"], acc=(h == 1))
                    if h == 1:
                        kb.op("act", lambda e: e.copy(out=RU2[:], in_=PS[7][:, 128:256]), reads=["ps7"], writes=["ru"])
                for h in range(2):
                    q = qi(h, j)
                    hp = slice(64 * h, 64 * h + 64)
                    yo = PS[6][hp, js]
                    kb.op("pe", lambda e: e.matmul(yo, lhsT=RSB[hp, l, c, :], rhs=ARr[hp, j, 1, :], start=True, stop=False),
                          reads=["rsb%d" % c, "rt"], writes=["ps6"], acc=True)
                    kb.op("pe", lambda e: e.matmul(yo, lhsT=RU2[:, h * 64:h * 64 + 64], rhs=QT[q][:, 2, :], start=False, stop=False),
                          reads=["ru", "qtA%d" % q], writes=["ps6"], acc=True)
                    kb.op("pe", lambda e: e.matmul(yo, lhsT=VTM[:, j, hp], rhs=QT[q][:, 4, :], start=False, stop=True),
                          reads=["vtm", "qtK%d" % q], writes=["ps6"], acc=True)
                for h in range(2):
                    hp = slice(64 * h, 64 * h + 64)
                    to = PS[5][hp, 0:64]
                    kb.op("pe", lambda e: e.matmul(to, lhsT=BTM[:, j, hp], rhs=RU2[:, h * 64:h * 64 + 64], start=True, stop=False),
                          reads=["btm", "ru"], writes=["ps5"], acc=(h == 1))
                    kb.op("pe", lambda e: e.matmul(to, lhsT=KTM[:, j, hp], rhs=VTM[:, j, hp], start=False, stop=True),
                          reads=["ktm", "vtm"], writes=["ps5"], acc=True)
                kb.op("dve", lambda e: e.tensor_tensor(out=RTT[:], in0=PS[5][:, 0:64], in1=RS[:, l, c, :], op=ALU.add),
                      reads=["ps5", "rs%d" % c], writes=["rtt"])
                kb.op("dve", lambda e: e.tensor_scalar(out=RS[:, l, c, :], in0=RTT[:], scalar1=DCOL[:, j:j + 1], scalar2=None, op0=ALU.mult),
                      reads=["rtt", "dcol"], writes=["rs%d" % c])
                kb.op("act", lambda e: e.copy(out=RSB[:, l, c, :], in_=RS[:, l, c, :]), reads=["rs%d" % c], writes=["rsb%d" % c])
            kb.op("act", lambda e: e.copy(out=YS[:], in_=bank(6)), reads=["ps6"], writes=["ys"])
            pb = rbank()
            kb.op("pe", lambda e: e.matmul(bank(pb), lhsT=BLK64, rhs=YS[:], start=True, stop=True), reads=["cst", "ys"], writes=["ps%d" % pb])
            kb.op("dve", lambda e: e.scalar_tensor_tensor(out=YC[:], in0=bank(pb), scalar=-1.0 / 64, in1=YS[:], op0=ALU.mult, op1=ALU.add),
                  reads=["ps%d" % pb, "ys"], writes=["yc"])
            kb.op("act", lambda e: e.activation(out=YS[:].bitcast(BF16)[:, 0:TT], in_=YC[:], func=AF.Square), reads=["yc"], writes=["ys"])
            pb = rbank()
            kb.op("pe", lambda e: e.matmul(bank(pb), lhsT=BLKB[:], rhs=YS[:].bitcast(BF16)[:, 0:TT], start=True, stop=True), reads=["blkb", "ys"], writes=["ps%d" % pb])
            kb.op("act", lambda e: e.activation(out=YS[:], in_=bank(pb), func=AF.Ln, scale=1.0 / 64, bias=64e-5), reads=["ps%d" % pb], writes=["ys"])
            kb.op("act", lambda e: e.activation(out=YS[:], in_=YS[:], func=AF.Exp, scale=-0.5), reads=["ys"], writes=["ys"])
            kb.op("dve", lambda e: e.tensor_tensor(out=YC[:], in0=YC[:], in1=YS[:], op=ALU.mult), reads=["yc", "ys"], writes=["yc"])
            kb.op("dve", lambda e: e.tensor_scalar(out=YC[:], in0=YC[:], scalar1=pv(l, "ln_w", c), scalar2=pv(l, "ln_b", c),
                                                   op0=ALU.mult, op1=ALU.add), reads=["yc", "pv"], writes=["yc"])
            kb.op("pool", lambda e: e.tensor_tensor(out=YC[:], in0=YC[:], in1=BON[:], op=ALU.add), reads=["yc", "bon"], writes=["yc"])
            kb.op("dve", lambda e: e.tensor_tensor(out=BR[1][:, c, :], in0=YC[:], in1=GATE[:], op=ALU.mult), reads=["yc", "gate"], writes=["br1"])

    def dump_br(l, it):
        for k in range(4):
            for c in range(4):
                kb.dma("pool", dbg_br[l, k, c, :, it * TT:(it + 1) * TT], BR[k][:, c, :], reads=["br%d" % k], writes=["dbg"])

    def dump_x(l, which, it):
        for j in range(NJ):
            kb.dma("sp", dbg_x[l, which, it * TT + j * 128: it * TT + (j + 1) * 128, :], X[:, j, :], reads=["X%d" % j], writes=["dbg"])

    for it in range(NT):
        for j in range(NJ):
            kb.dma("sp", X[:, j, :], x_d[it * TT + j * 128: it * TT + (j + 1) * 128, :], writes=["X%d" % j])
        for l in range(nlayers):
            rmsnorm_T(l, "g_mix", None)
            if "pool" in stages:
                pool_stage(l, it)
            else:
                zero_branch(0)
            if "rwkv" in stages:
                rwkv_stage(l, it)
            else:
                zero_branch(1)
            if "sg" in stages:
                sg_stage(l, it)
            else:
                zero_branch(2)
            if "hgrn" in stages:
                hgrn_stage(l, it)
            else:
                zero_branch(3)
            if debug:
                dump_br(l, it)
            merge_stage(l, it)
            if debug:
                dump_x(l, 0, it)
            if "ffn" in stages:
                ffn_stage(l, it)
            if debug:
                dump_x(l, 1, it)
            if "ple" in stages:
                ple_stage(l, it)
            if debug:
                dump_x(l, 2, it)
        gfin = BC[:, DEPTH * BR_W:DEPTH * BR_W + 1024]
        for j in range(NJ):
            kb.op("act", lambda e: e.activation(out=XN[j % 2][:], in_=X[:, j, :], func=AF.Square, accum_out=SS[:, j:j + 1]),
                  reads=["X%d" % j], writes=["xn%d" % (j % 2), "ss%d" % j])
            kb.op("act", lambda e: e.activation(out=SS[:, j:j + 1], in_=SS[:, j:j + 1], func=AF.Ln, scale=1.0 / D, bias=1e-6),
                  reads=["ss%d" % j], writes=["ss%d" % j])
            kb.op("act", lambda e: e.activation(out=SS[:, j:j + 1], in_=SS[:, j:j + 1], func=AF.Exp, scale=-0.5),
                  reads=["ss%d" % j], writes=["ss%d" % j])
            for hf in range(2):
                hsl = slice(hf * 512, (hf + 1) * 512)
                kb.op("dve", lambda e: e.scalar_tensor_tensor(out=TMP[hf][:], in0=X[:, j, hsl], scalar=SS[:, j:j + 1], in1=gfin[:, hsl],
                                                              op0=ALU.mult, op1=ALU.mult),
                      reads=["X%d" % j, "ss%d" % j, "bc"], writes=["tmp%d" % hf])
                kb.dma("sp", out_d[it * TT + j * 128: it * TT + (j + 1) * 128, hsl], TMP[hf][:], reads=["tmp%d" % hf], writes=["out"])
    for e in ("sp", "pool", "act"):
        kb.wait_all(e)
    print("instructions", kb.n_inst, "waits", kb.n_wait, "arena max words", max(amax[0], aoff[0]))
    kb.close()
    return nc


def make_in_maps(inputs, S, ncores):
    cst = make_consts()
    pvv = pack_pv(inputs)
    bc = pack_bc(inputs)
    sm = pack_small(inputs)
    f = lambda a: np.ascontiguousarray(np.asarray(a, np.float32))
    shared = {
        "w_in": f(inputs["w_in"]), "w_branch": f(inputs["w_branch"]).reshape(DEPTH, 4 * MIXW, D), "w_out": f(inputs["w_out"]),
        "ffn_up": f(inputs["ffn_up"]), "ffn_down": f(inputs["ffn_down"]), "ple_proj": f(inputs["ple_proj"]),
        "ple_gate": f(inputs["ple_gate"]), "cst": cst, "pv": pvv, "bc": bc, "sm": sm, "sgb": pack_sgb(inputs),
    }
    maps = []
    for b in range(ncores):
        m = dict(shared)
        m["x"] = f(inputs["x"][b])
        m["p"] = f(inputs["p"][:, b])
        maps.append(m)
    return maps


def kernel(**inputs):
    inputs = {k: np.asarray(v) for k, v in inputs.items()}
    B, S, _ = inputs["x"].shape
    nc = build_nc(S)
    maps = make_in_maps(inputs, S, B)
    res = run_bass_kernel_spmd(nc, maps, core_ids=list(range(B)))
    return np.stack([r["out"] for r in res.results], axis=0).astype(np.float32)
```

```python
import contextlib
import numpy as np
import concourse.bass as bass
import concourse.mybir as mybir
from concourse.bass_utils import run_bass_kernel_spmd

F32 = mybir.dt.float32
BF16 = mybir.dt.bfloat16
F32R = mybir.dt.float32r
RDT = BF16
AF = mybir.ActivationFunctionType
ALU = mybir.AluOpType

D = 1024
DEPTH = 2
MIXW = 512
D_IN = 8960
DFF = 2816
PLE = 256
C_A, C_B, C_C, C_D, C_G = 0, 512, 2304, 3328, 4864
LAMB = float(np.exp(-0.5))


class KB:
    def __init__(self, nc, n_dma_sems=32):
        self.nc = nc
        self.es = contextlib.ExitStack()
        self.eng = {"pe": nc.tensor, "dve": nc.vector, "act": nc.scalar, "pool": nc.gpsimd, "sp": nc.sync}
        self.sem = {}
        self.cnt = {}
        for name in self.eng:
            self.sem[name] = self.es.enter_context(nc.semaphore("s_" + name))
            self.cnt[name] = 0
        self.dma_sems = [self.es.enter_context(nc.semaphore("d%d" % i)) for i in range(2 * n_dma_sems)]
        self.dma_val = [0] * (2 * n_dma_sems)
        self.n_dma_sems = n_dma_sems
        self.dma_rr = {"hw": 0, "sw": 0}
        self.seen = {name: {} for name in self.eng}
        self.last_w = {}
        self.readers = {}
        self.n_inst = 0
        self.n_wait = 0
        self.alias = {}

    def _semh(self, semkey):
        if isinstance(semkey, tuple):
            return self.dma_sems[semkey[1]]
        return self.sem[semkey]

    def _wait(self, e, tok):
        semkey, val = tok
        if self.seen[e].get(semkey, 0) >= val:
            return
        self.eng[e].wait_ge(self._semh(semkey), val)
        self.seen[e][semkey] = val
        self.n_wait += 1

    def _deps(self, e, reads, writes, acc=False):
        toks = {}

        def add(tok):
            if tok is None:
                return
            k, v = tok
            if toks.get(k, 0) < v:
                toks[k] = v
        for k in reads:
            add(self.last_w.get(k))
        for k in writes:
            lw = self.last_w.get(k)
            if not (acc and lw is not None and lw[0] == e):
                add(lw)
            for t in self.readers.get(k, ()):
                add(t)
        for k, v in toks.items():
            self._wait(e, (k, v))

    def _commit(self, tok, reads, writes):
        for k in reads:
            lst = self.readers.setdefault(k, [])
            lst.append(tok)
            if len(lst) > 16:
                d = {}
                for (sk, v) in lst:
                    if d.get(sk, 0) < v:
                        d[sk] = v
                self.readers[k] = list(d.items())
        for k in writes:
            self.last_w[k] = tok
            self.readers[k] = []

    def _x(self, keys):
        al = self.alias
        if not al:
            return keys
        out = []
        for k in keys:
            out.extend(al.get(k, (k,)))
        return out

    def op(self, e, fn, reads=(), writes=(), acc=False):
        reads, writes = self._x(reads), self._x(writes)
        self._deps(e, reads, writes, acc)
        ins = fn(self.eng[e])
        self.cnt[e] += 1
        ins.then_inc(self.sem[e], 1)
        self._commit((e, self.cnt[e]), reads, writes)
        self.n_inst += 1
        return ins

    def dma(self, q, out, in_, reads=(), writes=(), **kw):
        reads, writes = self._x(reads), self._x(writes)
        self._deps(q, reads, writes)
        kind = "sw" if q == "pool" else "hw"
        i = self.dma_rr[kind] + (self.n_dma_sems if kind == "sw" else 0)
        self.dma_rr[kind] = (self.dma_rr[kind] + 1) % self.n_dma_sems
        if self.dma_val[i] > 0:
            self._wait(q, (("d", i), self.dma_val[i]))
        ins = self.eng[q].dma_start(out=out, in_=in_, **kw)
        self.dma_val[i] += 16
        ins.then_inc(self.dma_sems[i], 16)
        tok = (("d", i), self.dma_val[i])
        self._commit(tok, reads, writes)
        self.n_inst += 1
        return tok

    def wait_all(self, e):
        toks = {}
        for tok in self.last_w.values():
            k, v = tok
            if toks.get(k, 0) < v:
                toks[k] = v
        for k, v in toks.items():
            self._wait(e, (k, v))

    def close(self):
        self.es.close()


CST_COLS = 128 * 8 + 64


def make_consts():
    i = np.arange(128)[:, None]
    j = np.arange(128)[None, :]
    same = (i // 32) == (j // 32)
    mats = [
        (i == j),
        (i < j),
        (i <= j),
        (i > j),
        (i <= j) & same,
        (i > j) & same,
        (i // 64) == (j // 64),
        np.ones((128, 128), bool),
    ]
    c = np.concatenate([m.astype(np.float32) for m in mats], axis=1)
    fix = np.zeros((128, 4, 16), np.float32)
    for g, w in enumerate((2, 4, 8, 16)):
        t = np.arange(16)
        fix[:, g, :] = (w / np.minimum(t + 1, w))[None, :]
    return np.ascontiguousarray(np.concatenate([c, fix.reshape(128, 64)], axis=1))


PV_L = {}
_o = 0
for _n, _w in [("g_mix", 8), ("g_ffn", 8), ("g_ple", 8), ("pool_scale", 4), ("mu", 14), ("w0", 4), ("a0", 4),
               ("kk", 4), ("ka", 4), ("rk", 4), ("ln_w", 4), ("ln_b", 4), ("hnorm", 4), ("cw0", 44), ("cw1", 44),
               ("cw2", 44), ("cb", 44), ("lb_a", 4), ("lb_b", 4)]:
    PV_L[_n] = (_o, _w)
    _o += _w
PV_LW = _o


def fm(v, n):
    return np.ascontiguousarray(np.asarray(v, np.float32).reshape(n, 128).T)


def pack_pv(inp):
    out = np.zeros((128, DEPTH * PV_LW), np.float32)
    for l in range(DEPTH):
        def put(name, arr):
            o, w = PV_L[name]
            out[:, l * PV_LW + o: l * PV_LW + o + w] = arr
        put("g_mix", fm(inp["norm_mix"][l], 8))
        put("g_ffn", fm(inp["norm_ffn"][l], 8))
        put("g_ple", fm(inp["norm_ple"][l], 8))
        put("pool_scale", fm(inp["pool_scale"][l], 4))
        put("mu", fm(inp["rwkv_mu"][l], 14))
        put("w0", fm(inp["rwkv_w0"][l], 4))
        put("a0", fm(inp["rwkv_a0"][l], 4))
        put("kk", fm(inp["rwkv_kk"][l], 4))
        put("ka", fm(inp["rwkv_ka"][l], 4))
        put("rk", fm(inp["rwkv_rk"][l].reshape(-1), 4))
        put("ln_w", fm(inp["rwkv_ln_w"][l], 4))
        put("ln_b", fm(inp["rwkv_ln_b"][l], 4))
        put("hnorm", fm(inp["hgrn_norm"][l], 4))
        for j in range(3):
            put("cw%d" % j, fm(inp["ffn_conv"][l, j], 44))
        put("cb", fm(inp["ffn_conv_b"][l], 44))
        put("lb_a", fm(inp["hgrn_lb"][0], 4))
        put("lb_b", fm(inp["hgrn_lb"][1], 4))
    return out


BR_W = 512 + 512
BC_COLS = DEPTH * BR_W + 1024


def pack_bc(inp):
    rows = []
    for l in range(DEPTH):
        rows.append(np.concatenate([inp["sg_ln_w"][l], inp["sg_ln_b"][l]]).astype(np.float32))
    rows.append(np.asarray(inp["norm_final"], np.float32))
    r = np.concatenate(rows)
    return np.ascontiguousarray(np.broadcast_to(r[None, :], (128, r.shape[0])))


def pack_sgb(inp):
    return np.ascontiguousarray(np.asarray(inp["sg_b"], np.float32).reshape(1, DEPTH * 512))


def pack_small(inp):
    outs = []
    for l in range(DEPTH):
        pw = np.transpose(inp["pool_w"][l], (1, 0, 2)).reshape(128, 512)
        sw = np.transpose(inp["sg_w"][l], (1, 0, 2)).reshape(128, 512)
        w2a2 = np.concatenate([inp["rwkv_w2"][l], inp["rwkv_a2"][l]], axis=0)
        g2 = inp["rwkv_g2"][l]
        outs.append(np.concatenate([pw, sw, w2a2, g2], axis=1))
    return np.ascontiguousarray(np.concatenate(outs, axis=1).astype(np.float32))


SM_W = 2048


def build_nc(S, TT=512, nlayers=DEPTH, debug=False, stages=("pool", "rwkv", "sg", "hgrn", "ffn", "ple")):
    NJ = TT // 128
    NT = S // TT
    nc = bass.Bass("TRN2", target_bir_lowering=False)
    dt = nc.dram_tensor
    x_d = dt("x", [S, D], F32, kind="ExternalInput").ap()
    p_d = dt("p", [DEPTH, S, PLE], F32, kind="ExternalInput").ap()
    w_in_d = dt("w_in", [DEPTH, D, D_IN], F32, kind="ExternalInput").ap()
    w_br_d = dt("w_branch", [DEPTH, 4 * MIXW, D], F32, kind="ExternalInput").ap()
    w_out_d = dt("w_out", [DEPTH, D, D], F32, kind="ExternalInput").ap()
    w_up_d = dt("ffn_up", [DEPTH, D, 2 * DFF], F32, kind="ExternalInput").ap()
    w_dn_d = dt("ffn_down", [DEPTH, DFF, D], F32, kind="ExternalInput").ap()
    w_pp_d = dt("ple_proj", [DEPTH, PLE, D], F32, kind="ExternalInput").ap()
    w_pg_d = dt("ple_gate", [DEPTH, D, D], F32, kind="ExternalInput").ap()
    cst_d = dt("cst", [128, CST_COLS], F32, kind="ExternalInput").ap()
    pv_d = dt("pv", [128, DEPTH * PV_LW], F32, kind="ExternalInput").ap()
    bc_d = dt("bc", [128, BC_COLS], F32, kind="ExternalInput").ap()
    sgb_d = dt("sgb", [1, DEPTH * 512], F32, kind="ExternalInput").ap()
    sm_d = dt("sm", [128, DEPTH * SM_W], F32, kind="ExternalInput").ap()
    out_d = dt("out", [S, D], F32, kind="ExternalOutput").ap()
    def _blocks():
        b = {}
        in_cols = [(C_A, 512)] + [(C_B + g * 512, 512) for g in range(3)] + [(C_B + 1536, 256)] + \
                  [(C_C, 512), (C_C + 512, 512)] + [(C_D + i * 512, 512) for i in range(3)] + \
                  [(C_G + i * 512, 512) for i in range(8)]
        b["in"] = [(0, 8, c0, w) for c0, w in in_cols]
        b["br"] = [(k * 512, 4, h * 512, 512) for k in range(4) for h in range(2)]
        b["out"] = [(0, 8, h * 512, 512) for h in range(2)]
        up = []
        for gq in range(6):
            nch = min(4, 22 - 4 * gq)
            up += [(0, 8, gq * 512, nch * 128), (0, 8, DFF + gq * 512, nch * 128)]
        b["up"] = up
        b["dn"] = [(kg * 1024, 8 if kg < 2 else 6, h * 512, 512) for h in range(2) for kg in range(3)]
        b["pp"] = [(0, 2, h * 512, 512) for h in range(2)]
        b["pg"] = [(0, 8, h * 512, 512) for h in range(2)]
        return b
    WBLK = _blocks()
    WOFF = {}
    WTOT = {}
    for _nm, _lst in WBLK.items():
        _o = 0
        for _i, (_r0, _nk, _c0, _w) in enumerate(_lst):
            WOFF[(_nm, _r0, _c0)] = (_o, _nk, _w, _i)
            _o += _nk * 128 * _w
        WTOT[_nm] = _o
    WSRC = {"in": w_in_d, "br": w_br_d, "out": w_out_d, "up": w_up_d, "dn": w_dn_d, "pp": w_pp_d, "pg": w_pg_d}
    SCR = {nm: dt("s_" + nm, [DEPTH, WTOT[nm]], BF16, kind="Internal").ap() for nm in WBLK}
    if debug:
        dbg_br = dt("dbg_br", [DEPTH, 4, 4, 128, S], F32, kind="ExternalOutput").ap()
        dbg_x = dt("dbg_x", [DEPTH, 3, S, D], F32, kind="ExternalOutput").ap()

    kb = KB(nc)
    es = kb.es

    def sb(name, shape, dtype=F32):
        return es.enter_context(nc.sbuf_tensor(name, shape, dtype))

    ARENA_WORDS = 33 * TT
    ARENA = es.enter_context(nc.sbuf_tensor("arena", [128, ARENA_WORDS], F32))
    aoff = [0]

    amax = [0]

    def arena_reset():
        amax[0] = max(amax[0], aoff[0])
        for e_ in ("pe", "dve", "act", "pool", "sp"):
            kb.wait_all(e_)
        aoff[0] = 0

    def ab(name, shape, dtype=F32):
        n = int(np.prod(shape[1:]))
        words = n if dtype in (F32, F32R) else (n + 1) // 2
        o = aoff[0]
        aoff[0] += words
        assert aoff[0] <= ARENA_WORDS, (name, aoff[0])
        v = ARENA[:, o:o + words]
        if dtype != F32:
            v = v.bitcast(dtype)
        if len(shape) == 2:
            return v
        if len(shape) == 3:
            v = v.rearrange("p (a b) -> p a b", a=shape[1])
        elif len(shape) == 4:
            v = v.rearrange("p (a b c) -> p a b c", a=shape[1], b=shape[2])
        return v

    PS = [es.enter_context(nc.psum_tensor("ps%d" % i, [128, 512], F32)) for i in range(8)]

    CST = sb("cst_t", [128, CST_COLS])
    PV = sb("pv_t", [128, DEPTH * PV_LW])
    BC = sb("bc_t", [128, BC_COLS])
    SMF = ab("smf_t", [128, SM_W])
    kb.dma("sp", CST[:], cst_d, writes=["cst"])
    kb.dma("sp", PV[:], pv_d, writes=["pv"])
    kb.dma("sp", BC[:], bc_d, writes=["bc"])
    IDENT = CST[:, 0:128]
    M_LT = CST[:, 128:256]
    M_LE = CST[:, 256:384]
    M_GT = CST[:, 384:512]
    MH_LE = CST[:, 512:640]
    MH_GT = CST[:, 640:768]
    BLK64 = CST[:, 768:896]
    ONES = CST[:, 896:1024]
    IDB = sb("idb", [128, 128], BF16)
    ONEB = sb("oneb", [128, 128], BF16)
    kb.op("dve", lambda e: e.tensor_copy(out=IDB[:], in_=IDENT), reads=["cst"], writes=["idb"])
    kb.op("dve", lambda e: e.tensor_copy(out=ONEB[:], in_=ONES), reads=["cst"], writes=["oneb"])
    BLKB = sb("blkb", [128, 128], BF16)
    kb.op("dve", lambda e: e.tensor_copy(out=BLKB[:], in_=BLK64), reads=["cst"], writes=["blkb"])

    def pv(l, name, c=None):
        o, w = PV_L[name]
        o += l * PV_LW
        if c is None:
            return PV[:, o:o + w]
        return PV[:, o + c:o + c + 1]

    for l in range(nlayers):
        for nm, lst in WBLK.items():
            for (r0, nk, c0, w) in lst:
                off, _, _, bi = WOFF[(nm, r0, c0)]
                dst = SCR[nm][l, off:off + nk * 128 * w].rearrange("(p k n) -> p k n", p=128, k=nk)
                src = WSRC[nm][l, r0:r0 + nk * 128, c0:c0 + w].rearrange("(k p) n -> p k n", p=128)
                kb.dma("pool", dst, src, writes=["W%s%d_%d" % (nm, l, bi)])

    NSLOT = 3
    WB = [sb("wb%d" % i, [128, 4096], BF16) for i in range(NSLOT)]
    wrr = [0]
    for _i in range(NSLOT):
        kb.alias["wb%d" % _i] = ("wb%da" % _i, "wb%db" % _i)

    def load_w(name, l, src3, kc, ncol, rows=None, slot=None):
        ap2, ckey = src3
        if slot is None:
            i = wrr[0]
            wrr[0] = (i + 1) % NSLOT
            o, key = 0, "wb%d" % i
        else:
            i, hf = slot
            o, key = (0, "wb%d" % i) if hf is None else (hf * 2048, "wb%d%s" % (i, "ab"[hf]))
        kb.dma("sp", WB[i][:, o:o + kc * ncol], ap2, reads=[ckey], writes=[key])
        view = WB[i][:, o:o + kc * ncol].rearrange("p (k n) -> p k n", k=kc)
        return view, key

    def wview(nm, l, r0, nk, c0, ncol):
        off, nk_, w_, bi = WOFF[(nm, r0, c0)]
        assert nk_ == nk and w_ == ncol, (nm, r0, c0, nk, ncol)
        ap2 = SCR[nm][l, off:off + nk * 128 * ncol].rearrange("(p x) -> p x", p=128)
        return ap2, "W%s%d_%d" % (nm, l, bi)

    X = sb("X", [128, NJ, D])
    HT = sb("HT", [128, 8, TT], BF16)
    XN = [sb("XN%d" % i, [128, D], BF16) for i in range(2)]
    SS = sb("ss", [128, 8])
    BR = [sb("BR%d" % k, [128, 4, TT], BF16) for k in range(4)]
    TMP = [sb("TMP%d" % i, [128, 512]) for i in range(2)]

    POOLW = sb("poolw", [128, DEPTH, 512], BF16)
    SGWT = sb("sgwt", [128, DEPTH, 512], BF16)
    W2A2 = sb("w2a2", [128, DEPTH, 512], BF16)
    G2 = sb("g2", [128, DEPTH, 512], BF16)
    SGB = sb("sgb_t", [128, DEPTH, 512], BF16)
    for l in range(nlayers):
        kb.dma("sp", SMF[:], sm_d[:, l * SM_W:(l + 1) * SM_W], writes=["smf"])
        kb.op("dve", lambda e: e.tensor_copy(out=POOLW[:, l, :], in_=SMF[:, 0:512]), reads=["smf"], writes=["poolw"])
        kb.op("dve", lambda e: e.tensor_copy(out=W2A2[:, l, :], in_=SMF[:, 1024:1536]), reads=["smf"], writes=["w2a2"])
        kb.op("dve", lambda e: e.tensor_copy(out=G2[:, l, :], in_=SMF[:, 1536:2048]), reads=["smf"], writes=["g2"])
        kb.dma("sp", SMF[0:1, 0:512], sgb_d[:, l * 512:(l + 1) * 512], reads=["poolw", "w2a2", "g2", "sgwt"], writes=["smf"])
        kb.op("dve", lambda e: e.tensor_copy(out=SGB[0:1, l, :], in_=SMF[0:1, 0:512]), reads=["smf"], writes=["sgb"])
        for g in range(4):
            kb.op("pe", lambda e: e.transpose(out=PS[0][:, 0:128], in_=SMF[:, 512 + g * 128:512 + (g + 1) * 128],
                                              identity=IDENT), reads=["smf", "cst"], writes=["ps0"])
            kb.op("dve", lambda e: e.tensor_tensor(out=SGWT[:, l, g * 128:(g + 1) * 128], in0=PS[0][:, 0:128],
                                                   in1=M_LE, op=ALU.mult), reads=["ps0", "cst"], writes=["sgwt"])

    LBT = sb("lbt", [128, DEPTH, 4])
    OML = sb("oml", [128, DEPTH, 4])
    OMKA = sb("omka", [128, DEPTH, 4])
    kb.op("dve", lambda e: e.tensor_tensor(out=LBT[:, 1, :], in0=pv(0, "lb_b"), in1=pv(0, "lb_a"), op=ALU.subtract),
          reads=["pv"], writes=["lbt"])
    kb.op("act", lambda e: e.activation(out=LBT[:, 1, :], in_=LBT[:, 1, :], func=AF.Sigmoid), reads=["lbt"], writes=["lbt"])
    kb.op("dve", lambda e: e.memset(LBT[:, 0, :], 0.0), reads=["lbt"], writes=["lbt"])
    kb.op("dve", lambda e: e.tensor_scalar(out=OML[:], in0=LBT[:], scalar1=-1.0, scalar2=1.0, op0=ALU.mult, op1=ALU.add),
          reads=["lbt"], writes=["oml"])
    for l in range(nlayers):
        kb.op("dve", lambda e: e.tensor_scalar(out=OMKA[:, l, :], in0=pv(l, "ka"), scalar1=-1.0, scalar2=1.0,
                                               op0=ALU.mult, op1=ALU.add), reads=["pv"], writes=["omka"])

    POOLH = sb("poolh", [128, DEPTH, 4, 16])
    kb.op("pool", lambda e: e.memset(POOLH[:], 0.0), writes=["poolh"])


    def rmsnorm_T(gname_l, gname, tagread):
        for j in range(NJ):
            kb.op("act", lambda e: e.activation(out=XN[j % 2][:], in_=X[:, j, :], func=AF.Square, accum_out=SS[:, j:j + 1]),
                  reads=["X%d" % j], writes=["xn%d" % (j % 2), "ss%d" % j])
        kb.op("act", lambda e: e.activation(out=SS[:, 0:NJ], in_=SS[:, 0:NJ], func=AF.Ln, scale=1.0 / D, bias=1e-6),
              reads=["ss%d" % j for j in range(NJ)], writes=["ss%d" % j for j in range(NJ)])
        kb.op("act", lambda e: e.activation(out=SS[:, 0:NJ], in_=SS[:, 0:NJ], func=AF.Exp, scale=-0.5),
              reads=["ss%d" % j for j in range(NJ)], writes=["ss%d" % j for j in range(NJ)])
        for j in range(NJ):
            xn = XN[j % 2]
            kb.op("dve", lambda e: e.tensor_scalar(out=xn[:], in0=X[:, j, :], scalar1=SS[:, j:j + 1], scalar2=None,
                                                   op0=ALU.mult), reads=["X%d" % j, "ss%d" % j], writes=["xn%d" % (j % 2)])
            pstE = PS[7][:].bitcast(BF16)
            pstO = PS[6][:].bitcast(BF16)
            for c in range(8):
                pst_c = (pstO if c % 2 else pstE)[:, (c // 2) * 128:(c // 2 + 1) * 128]
                kb.op("pe", lambda e: e.transpose(out=pst_c, in_=xn[:, c * 128:(c + 1) * 128],
                                                  identity=IDB[:]), reads=["xn%d" % (j % 2), "idb"], writes=["ps6" if c % 2 else "ps7"], acc=True)
            for c in range(8):
                pst_c = (pstO if c % 2 else pstE)[:, (c // 2) * 128:(c // 2 + 1) * 128]
                kb.op("act" if c % 2 else "dve",
                      (lambda e: e.activation(out=HT[:, c, j * 128:(j + 1) * 128], in_=pst_c,
                                              func=AF.Copy, scale=pv(gname_l, gname, c))) if c % 2 else
                      (lambda e: e.tensor_scalar(out=HT[:, c, j * 128:(j + 1) * 128], in0=pst_c,
                                                 scalar1=pv(gname_l, gname, c), scalar2=None, op0=ALU.mult)),
                      reads=["ps6" if c % 2 else "ps7", "pv"], writes=["HT%d" % c])

    def proj_fm(l, wv, wkey, ci, ps, pskey, nk=8, rhs=None, rkey="HT"):
        rhs = HT if rhs is None else rhs
        for k in range(nk):
            kb.op("pe", lambda e: e.matmul(ps, lhsT=wv[:, k, ci * 128:(ci + 1) * 128], rhs=rhs[:, k, :],
                                           start=(k == 0), stop=(k == nk - 1)),
                  reads=[wkey, (rkey + str(k)) if rkey == "HT" else rkey], writes=[pskey], acc=True)

    def proj_tm(l, wv, wkey, j, ps, pskey, ncol, nk=8, lhs=None, lkey="HT"):
        lhs = HT if lhs is None else lhs
        for k in range(nk):
            kb.op("pe", lambda e: e.matmul(ps, lhsT=lhs[:, k, j * 128:(j + 1) * 128], rhs=wv[:, k, 0:ncol],
                                           start=(k == 0), stop=(k == nk - 1)),
                  reads=[wkey, (lkey + str(k)) if lkey == "HT" else lkey], writes=[pskey], acc=True)

    def bank(i, w=None):
        w = TT if w is None else w
        return PS[i][:, 0:w]

    def pool_stage(l, it):
        arena_reset()
        XA = ab("xa", [128, 4, 16 + TT])
        PD = ab("pd", [128, TT], BF16)
        PSUMS = [ab("psum_s%d" % i, [128, 16 + TT]) for i in range(2)]
        wv, wk = load_w("in", l, wview("in", l, 0, 8, C_A, 512), 8, 512)
        wins = (2, 4, 8, 16)
        for g in range(4):
            pb = g % 2
            proj_fm(l, wv, wk, g, bank(pb), "ps%d" % pb)
            kb.op("pool", lambda e: e.tensor_copy(out=XA[:, g, 0:16], in_=POOLH[:, l, g, :]), reads=["poolh"], writes=["xa%d" % g])
            kb.op("act", lambda e: e.copy(out=XA[:, g, 16:16 + TT], in_=bank(pb)), reads=["ps%d" % pb], writes=["xa%d" % g])
            kb.op("pool", lambda e: e.tensor_copy(out=POOLH[:, l, g, :], in_=XA[:, g, TT:TT + 16]), reads=["xa%d" % g], writes=["poolh"])
            cur = XA[:, g, :]
            ckey = "xa%d" % g
            sh = 1
            bi = 0
            while sh < wins[g]:
                nxt = PSUMS[bi]
                nkey = "psums%d" % bi
                lo = 2 * sh - 1
                kb.op("dve", lambda e: e.tensor_tensor(out=nxt[:, lo:16 + TT], in0=cur[:, lo:16 + TT],
                                                       in1=cur[:, lo - sh:16 + TT - sh], op=ALU.add),
                      reads=[ckey], writes=[nkey])
                cur, ckey = nxt, nkey
                sh *= 2
                bi ^= 1
            dst = PSUMS[bi]
            dkey = "psums%d" % bi
            if it == 0:
                kb.op("dve", lambda e: e.tensor_tensor(out=cur[:, 16:32], in0=cur[:, 16:32],
                                                       in1=CST[:, 1024 + g * 16:1024 + (g + 1) * 16], op=ALU.mult),
                      reads=[ckey, "cst"], writes=[ckey])
            kb.op("dve", lambda e: e.scalar_tensor_tensor(out=PD[:], in0=cur[:, 16:16 + TT], scalar=1.0 / wins[g],
                                                          in1=XA[:, g, 16:16 + TT], op0=ALU.mult, op1=ALU.subtract),
                  reads=[ckey, "xa%d" % g], writes=["pd"])
            kb.op("pe", lambda e: e.matmul(bank(2 + pb), lhsT=POOLW[:, l, g * 128:(g + 1) * 128], rhs=PD[:], start=True, stop=True),
                  reads=["poolw", "pd"], writes=["ps%d" % (2 + pb)])
            kb.op("act", lambda e: e.activation(out=BR[0][:, g, :], in_=bank(2 + pb), func=AF.Copy, scale=pv(l, "pool_scale", g)),
                  reads=["ps%d" % (2 + pb), "pv"], writes=["br0"])

    def sg_stage(l, it):
        arena_reset()
        SGU = ab("sgu", [128, 4, TT], BF16)
        SGV = ab("sgv", [128, NJ, 512])
        SGVN = [ab("sgvn%d" % i, [128, 512], BF16) for i in range(2)]
        BNS = ab("bns", [128, NJ, 8])
        BNA = ab("bna", [128, NJ, 4])
        wv, wk = load_w("in", l, wview("in", l, 0, 8, C_C, 512), 8, 512)
        for c in range(4):
            pb = c % 2
            proj_fm(l, wv, wk, c, bank(pb), "ps%d" % pb)
            kb.op("act", lambda e: e.activation(out=SGU[:, c, :], in_=bank(pb), func=AF.Gelu_apprx_tanh),
                  reads=["ps%d" % pb], writes=["sgu"])
        wv, wk = load_w("in", l, wview("in", l, 0, 8, C_C + 512, 512), 8, 512)
        lnw = BC[:, l * BR_W:l * BR_W + 512]
        lnb = BC[:, l * BR_W + 512:l * BR_W + 1024]
        for j in range(NJ):
            pb = 2 + (j % 2)
            proj_tm(l, wv, wk, j, PS[pb][:, 0:512], "ps%d" % pb, 512)
            kb.op("act", lambda e: e.activation(out=SGV[:, j, :], in_=PS[pb][:, 0:512], func=AF.Gelu_apprx_tanh),
                  reads=["ps%d" % pb], writes=["sgv%d" % j])
            kb.op("dve", lambda e: e.bn_stats(out=BNS[:, j, 0:6], in_=SGV[:, j, :]), reads=["sgv%d" % j], writes=["bns%d" % j])
            kb.op("dve", lambda e: e.bn_aggr(out=BNA[:, j, 0:2], in_=BNS[:, j, 0:6]), reads=["bns%d" % j], writes=["bna%d" % j])
        bk = ["bna%d" % j for j in range(NJ)]
        kb.op("act", lambda e: e.activation(out=BNA[:, :, 2], in_=BNA[:, :, 1], func=AF.Ln, bias=1e-5), reads=bk, writes=bk)
        kb.op("act", lambda e: e.activation(out=BNA[:, :, 2], in_=BNA[:, :, 2], func=AF.Exp, scale=-0.5), reads=bk, writes=bk)
        for j in range(NJ):
            sgvn = SGVN[j % 2]
            nk_ = "sgvn%d" % (j % 2)
            kb.op("dve", lambda e: e.tensor_scalar(out=SGV[:, j, :], in0=SGV[:, j, :], scalar1=BNA[:, j, 0:1], scalar2=BNA[:, j, 2:3],
                                                   op0=ALU.subtract, op1=ALU.mult), reads=["sgv%d" % j, "bna%d" % j], writes=["sgv%d" % j])
            kb.op("dve", lambda e: e.tensor_tensor(out=SGV[:, j, :], in0=SGV[:, j, :], in1=lnw, op=ALU.mult), reads=["sgv%d" % j, "bc"], writes=["sgv%d" % j])
            kb.op("pool", lambda e: e.tensor_tensor(out=sgvn[:], in0=SGV[:, j, :], in1=lnb, op=ALU.add), reads=["sgv%d" % j, "bc"], writes=[nk_])
            for g in range(4):
                pg = 4 + g
                o = PS[pg][:, j * 128:(j + 1) * 128]
                kb.op("pe", lambda e: e.matmul(o, lhsT=sgvn[:, g * 128:(g + 1) * 128], rhs=SGWT[:, l, g * 128:(g + 1) * 128],
                                               start=True, stop=False), reads=[nk_, "sgwt"], writes=["ps%d" % pg], acc=True)
                kb.op("pe", lambda e: e.matmul(o, lhsT=ONEB[0:1, :], rhs=SGB[0:1, l, g * 128:(g + 1) * 128],
                                               start=False, stop=True), reads=["oneb", "sgb"], writes=["ps%d" % pg], acc=True)
        for g in range(4):
            kb.op("dve", lambda e: e.tensor_tensor(out=BR[2][:, g, :], in0=bank(4 + g), in1=SGU[:, g, :], op=ALU.mult),
                  reads=["ps%d" % (4 + g), "sgu"], writes=["br2"])

    def zero_branch(k):
        kb.op("pool", lambda e: e.memset(BR[k][:], 0.0), writes=["br%d" % k])

    def merge_stage(l, it):
        arena_reset()
        ACC = ab("ACC", [128, 8, TT])
        MT = ab("MT", [128, 8, TT], BF16)
        GS = [ab("GS%d" % i, [128, TT]) for i in range(2)]
        for k in range(4):
            for half in range(2):
                wg_v, wg_k = load_w("in", l, wview("in", l, 0, 8, C_G + k * 1024 + half * 512, 512), 8, 512, slot=(half, None))
                wb_v, wb_k = load_w("br", l, wview("br", l, k * 512, 4, half * 512, 512), 4, 512, slot=(2, half))
                for dq in range(4):
                    dc = half * 4 + dq
                    pa, pb = dq % 2, 2 + dq % 2
                    proj_fm(l, wg_v, wg_k, dq, bank(pa), "ps%d" % pa)
                    gs = GS[dq % 2]
                    kb.op("act", lambda e: e.activation(out=gs[:], in_=bank(pa), func=AF.Sigmoid),
                          reads=["ps%d" % pa], writes=["gs%d" % (dq % 2)])
                    proj_fm(l, wb_v, wb_k, dq, bank(pb), "ps%d" % pb, nk=4, rhs=BR[k], rkey="br%d" % k)
                    if k == 0:
                        kb.op("dve", lambda e: e.tensor_tensor(out=ACC[:, dc, :], in0=bank(pb), in1=gs[:], op=ALU.mult),
                              reads=["ps%d" % pb, "gs%d" % (dq % 2)], writes=["acc%d" % dc])
                    else:
                        kb.op("dve", lambda e: e.tensor_tensor(out=gs[:], in0=bank(pb), in1=gs[:], op=ALU.mult),
                              reads=["ps%d" % pb, "gs%d" % (dq % 2)], writes=["gs%d" % (dq % 2)])
                        if k < 3:
                            kb.op("pool", lambda e: e.tensor_tensor(out=ACC[:, dc, :], in0=ACC[:, dc, :], in1=gs[:], op=ALU.add),
                                  reads=["acc%d" % dc, "gs%d" % (dq % 2)], writes=["acc%d" % dc])
                        else:
                            kb.op("pool", lambda e: e.tensor_tensor(out=MT[:, dc, :], in0=ACC[:, dc, :], in1=gs[:], op=ALU.add),
                                  reads=["acc%d" % dc, "gs%d" % (dq % 2)], writes=["mt"])
        add_tm("out", l, None, 8, MT, "mt")

    def add_tm(name, l, s, nk, lhs, lkey):
        for half in range(2):
            wv, wk = load_w(name, l, wview(name, l, 0, nk, half * 512, 512), nk, 512)
            for j in range(NJ):
                pb = (half * NJ + j) % 4
                proj_tm(l, wv, wk, j, PS[pb][:, 0:512], "ps%d" % pb, 512, nk=nk, lhs=lhs, lkey=lkey)
                kb.op("dve", lambda e: e.tensor_tensor(out=X[:, j, half * 512:(half + 1) * 512], in0=PS[pb][:, 0:512],
                                                       in1=X[:, j, half * 512:(half + 1) * 512], op=ALU.add),
                      reads=["ps%d" % pb, "X%d" % j], writes=["X%d" % j])


    CARRYU = sb("carryu", [128, DEPTH, 44, 2])
    kb.op("pool", lambda e: e.memset(CARRYU[:], 0.0), writes=["carryu"])

    def ffn_stage(l, it):
        arena_reset()
        ACTT = ab("actt", [128, 22, TT], BF16)
        GG = ab("gg", [128, TT])
        RAWU = [ab("rawu%d" % i, [128, 2 + TT]) for i in range(2)]
        CV = [ab("cv%d" % i, [128, TT]) for i in range(2)]
        rmsnorm_T(l, "g_ffn", None)
        n = 0
        for gq in range(6):
            nch = min(4, 22 - 4 * gq)
            wg = load_w("up", l, wview("up", l, 0, 8, gq * 512, nch * 128), 8, nch * 128)
            wvv = load_w("up", l, wview("up", l, 0, 8, DFF + gq * 512, nch * 128), 8, nch * 128)
            for q in range(nch):
                p = 4 * gq + q
                for which, (wv, wk), ci in ((0, wg, p), (1, wvv, 22 + p)):
                    pb = n % 4
                    n += 1
                    proj_fm(l, wv, wk, q, bank(pb), "ps%d" % pb)
                    raw = RAWU[which]
                    rk = "rawu%d" % which
                    kb.op("pool", lambda e: e.tensor_copy(out=raw[:, 0:2], in_=CARRYU[:, l, ci, :]), reads=["carryu"], writes=[rk])
                    kb.op("act", lambda e: e.copy(out=raw[:, 2:2 + TT], in_=bank(pb)), reads=["ps%d" % pb], writes=[rk])
                    kb.op("pool", lambda e: e.tensor_copy(out=CARRYU[:, l, ci, :], in_=raw[:, TT:TT + 2]), reads=[rk], writes=["carryu"])
                    cv = CV[which]
                    ck = "cv%d" % which
                    kb.op("act", lambda e: e.activation(out=cv[:], in_=bank(pb), func=AF.Identity, scale=pv(l, "cw2", ci),
                                                        bias=pv(l, "cb", ci)), reads=["ps%d" % pb, "pv"], writes=[ck])
                    kb.op("dve", lambda e: e.scalar_tensor_tensor(out=cv[:], in0=raw[:, 1:1 + TT], scalar=pv(l, "cw1", ci),
                                                                  in1=cv[:], op0=ALU.mult, op1=ALU.add),
                          reads=[rk, "pv", ck], writes=[ck])
                    kb.op("dve", lambda e: e.scalar_tensor_tensor(out=cv[:], in0=raw[:, 0:TT], scalar=pv(l, "cw0", ci),
                                                                 in1=cv[:], op0=ALU.mult, op1=ALU.add),
                          reads=[rk, "pv", ck], writes=[ck])
                kb.op("act", lambda e: e.activation(out=GG[:], in_=CV[0][:], func=AF.Gelu_apprx_tanh), reads=["cv0"], writes=["gg"])
                kb.op("dve", lambda e: e.tensor_tensor(out=ACTT[:, p, :], in0=GG[:], in1=CV[1][:], op=ALU.mult),
                      reads=["gg", "cv1"], writes=["actt"])
        for half in range(2):
            for kg in range(3):
                nk = 8 if kg < 2 else 6
                wv, wk = load_w("dn", l, wview("dn", l, kg * 1024, nk, half * 512, 512), nk, 512)
                for j in range(NJ):
                    pb = 4 + (half * NJ + j) % 4
                    for k in range(nk):
                        kk_ = kg * 8 + k
                        kb.op("pe", lambda e: e.matmul(PS[pb][:, 0:512], lhsT=ACTT[:, kk_, j * 128:(j + 1) * 128], rhs=wv[:, k, :],
                                                       start=(kk_ == 0), stop=(kk_ == 21)),
                              reads=[wk, "actt"], writes=["ps%d" % pb], acc=True)
            for j in range(NJ):
                pb = 4 + (half * NJ + j) % 4
                kb.op("dve", lambda e: e.tensor_tensor(out=X[:, j, half * 512:(half + 1) * 512], in0=PS[pb][:, 0:512],
                                                       in1=X[:, j, half * 512:(half + 1) * 512], op=ALU.add),
                      reads=["ps%d" % pb, "X%d" % j], writes=["X%d" % j])


    def ple_stage(l, it):
        PTM = ab("ptm", [128, NJ, PLE])
        PTB = ab("ptb", [128, NJ, PLE], BF16)
        PT = ab("pt", [128, 2, TT], BF16)
        rmsnorm_T(l, "g_ple", None)
        for j in range(NJ):
            kb.dma("sp", PTM[:, j, :], p_d[l, it * TT + j * 128: it * TT + (j + 1) * 128, :], writes=["ptm"])
        kb.op("pool", lambda e: e.tensor_copy(out=PTB[:], in_=PTM[:]), reads=["ptm"], writes=["ptb"])
        pst = PS[7][:].bitcast(BF16)
        for j in range(NJ):
            for c in range(2):
                kb.op("pe", lambda e: e.transpose(out=pst[:, (j * 2 + c) * 128:(j * 2 + c + 1) * 128], in_=PTB[:, j, c * 128:(c + 1) * 128],
                                                  identity=IDB[:]), reads=["ptb", "idb"], writes=["ps7"], acc=True)
        for j in range(NJ):
            for c in range(2):
                kb.op("act", lambda e: e.copy(out=PT[:, c, j * 128:(j + 1) * 128], in_=pst[:, (j * 2 + c) * 128:(j * 2 + c + 1) * 128]),
                      reads=["ps7"], writes=["pt"])
        for half in range(2):
            wg = load_w("pg", l, wview("pg", l, 0, 8, half * 512, 512), 8, 512)
            wp = load_w("pp", l, wview("pp", l, 0, 2, half * 512, 512), 2, 512)
            for j in range(NJ):
                pa, pb = j % 2, 2 + j % 2
                proj_tm(l, wg[0], wg[1], j, PS[pa][:, 0:512], "ps%d" % pa, 512)
                tmp = TMP[j % 2]
                kb.op("act", lambda e: e.activation(out=tmp[:], in_=PS[pa][:, 0:512], func=AF.Sigmoid),
                      reads=["ps%d" % pa], writes=["tmp%d" % (j % 2)])
                proj_tm(l, wp[0], wp[1], j, PS[pb][:, 0:512], "ps%d" % pb, 512, nk=2, lhs=PT, lkey="pt")
                kb.op("dve", lambda e: e.tensor_tensor(out=tmp[:], in0=PS[pb][:, 0:512], in1=tmp[:], op=ALU.mult),
                      reads=["ps%d" % pb, "tmp%d" % (j % 2)], writes=["tmp%d" % (j % 2)])
                kb.op("pool", lambda e: e.tensor_tensor(out=X[:, j, half * 512:(half + 1) * 512], in0=tmp[:],
                                                        in1=X[:, j, half * 512:(half + 1) * 512], op=ALU.add),
                      reads=["tmp%d" % (j % 2), "X%d" % j], writes=["X%d" % j])

    OMLB = sb("omlb", [128, DEPTH, 512])
    DIAG = sb("diag", [128, 128])
    for l in range(nlayers):
        for h in range(4):
            kb.op("dve", lambda e: e.tensor_scalar(out=DIAG[:], in0=IDENT, scalar1=OML[:, l, h:h + 1], scalar2=None, op0=ALU.mult),
                  reads=["cst", "oml"], writes=["diag"])
            kb.op("pe", lambda e: e.matmul(PS[0][:, 0:128], lhsT=ONES, rhs=DIAG[:], start=True, stop=True),
                  reads=["cst", "diag"], writes=["ps0"])
            kb.op("act", lambda e: e.copy(out=OMLB[:, l, h * 128:(h + 1) * 128], in_=PS[0][:, 0:128]), reads=["ps0"], writes=["omlb"])
    HS = sb("hs", [128, DEPTH, 4, 128])
    HSB = sb("hsb", [128, DEPTH, 4, 128], BF16)
    kb.op("pool", lambda e: e.memset(HS[:], 0.0), writes=["hs0", "hs1", "hs2", "hs3"])
    kb.op("pool", lambda e: e.memset(HSB[:], 0.0), writes=["hsb0", "hsb1", "hsb2", "hsb3"])

    def hgrn_stage(l, it):
        arena_reset()
        HQ = ab("hq", [128, 4, TT])
        HKEY = ab("hkey", [128, 4, TT])
        HLG = ab("hlg", [128, NJ, 512])
        HKTM = ab("hktm", [128, NJ, 512])
        HV = ab("hv", [128, NJ, 512], BF16)
        HQT = ab("hqt", [128, 4, TT], BF16)
        HKT = ab("hkt", [128, 4, 128], BF16)
        HE = ab("he", [128, 4, 128])
        HE2 = ab("he2", [128, 4, 128])
        HDC = ab("hdc", [128, 4, 4])
        HKH = ab("hkh", [128, 512], BF16)
        HKH3 = ab("hkh3", [128, 512], BF16)
        HSC = ab("hsc", [128, 4, 128], BF16)
        HSQ = ab("hsq", [128, TT])
        HR = ab("hr", [128, TT])
        wq = load_w("in", l, wview("in", l, 0, 8, C_D, 512), 8, 512)
        wf = load_w("in", l, wview("in", l, 0, 8, C_D + 512, 512), 8, 512)
        wi = load_w("in", l, wview("in", l, 0, 8, C_D + 1024, 512), 8, 512)
        for h in range(4):
            pb = h % 2
            proj_fm(l, wq[0], wq[1], h, bank(pb), "ps%d" % pb)
            kb.op("act", lambda e: e.activation(out=HQ[:, h, :], in_=bank(pb), func=AF.Silu), reads=["ps%d" % pb], writes=["hq"])
            pb = 2 + h % 2
            proj_fm(l, wf[0], wf[1], h, bank(pb), "ps%d" % pb)
            kb.op("act", lambda e: e.activation(out=HKEY[:, h, :], in_=bank(pb), func=AF.Sigmoid, scale=-1.0),
                  reads=["ps%d" % pb], writes=["hkey%d" % h])
            kb.op("dve", lambda e: e.tensor_scalar(out=HKEY[:, h, :], in0=HKEY[:, h, :], scalar1=OML[:, l, h:h + 1], scalar2=None,
                                                   op0=ALU.mult), reads=["hkey%d" % h, "oml"], writes=["hkey%d" % h])
        for j in range(NJ):
            pb = j % 2
            proj_tm(l, wf[0], wf[1], j, PS[pb][:, 0:512], "ps%d" % pb, 512)
            kb.op("act", lambda e: e.activation(out=HKTM[:, j, :], in_=PS[pb][:, 0:512], func=AF.Sigmoid, scale=-1.0),
                  reads=["ps%d" % pb], writes=["hktm%d" % j])
            kb.op("dve", lambda e: e.tensor_tensor(out=HKTM[:, j, :], in0=HKTM[:, j, :], in1=OMLB[:, l, :], op=ALU.mult),
                  reads=["hktm%d" % j, "omlb"], writes=["hktm%d" % j])
            kb.op("dve", lambda e: e.tensor_scalar(out=HLG[:, j, :], in0=HKTM[:, j, :], scalar1=-1.0, scalar2=1.0,
                                                   op0=ALU.mult, op1=ALU.add), reads=["hktm%d" % j], writes=["hlg%d" % j])
            kb.op("dve", lambda e: e.tensor_scalar(out=HLG[:, j, :], in0=HLG[:, j, :], scalar1=1e-30, scalar2=None, op0=ALU.max),
                  reads=["hlg%d" % j], writes=["hlg%d" % j])
        for j in range(NJ):
            kb.op("act", lambda e: e.activation(out=HLG[:, j, :], in_=HLG[:, j, :], func=AF.Ln), reads=["hlg%d" % j], writes=["hlg%d" % j])
        for j in range(NJ):
            pb = 2 + j % 2
            proj_tm(l, wi[0], wi[1], j, PS[pb][:, 0:512], "ps%d" % pb, 512)
            kb.op("act", lambda e: e.copy(out=HV[:, j, :], in_=PS[pb][:, 0:512]), reads=["ps%d" % pb], writes=["hv%d" % j])
        for j in range(NJ):
            js = slice(j * 128, (j + 1) * 128)
            kb.op("pe", lambda e: e.matmul(PS[1][:, 0:512], lhsT=MH_GT, rhs=HLG[:, j, :], start=True, stop=True),
                  reads=["cst", "hlg%d" % j], writes=["ps1"])
            kb.op("act", lambda e: e.activation(out=TMP[0][:], in_=PS[1][:, 0:512], func=AF.Exp), reads=["ps1"], writes=["tmp0"])
            kb.op("dve", lambda e: e.tensor_tensor(out=HKH[:], in0=TMP[0][:], in1=HKTM[:, j, :], op=ALU.mult),
                  reads=["tmp0", "hktm%d" % j], writes=["hkh"])
            kb.op("dve", lambda e: e.tensor_scalar(out=HKH3[64:128, :], in0=HKH[64:128, :], scalar1=CST[64:128, 384 + 95:384 + 96],
                                                    scalar2=None, op0=ALU.mult), reads=["hkh", "cst"], writes=["hkh3"])
            for h in range(4):
                hs = slice(h * 128, (h + 1) * 128)
                kb.op("pe", lambda e: e.matmul(PS[0][:, hs], lhsT=HLG[:, j, hs], rhs=MH_LE, start=True, stop=True),
                      reads=["cst", "hlg%d" % j], writes=["ps0"], acc=True)
            kb.op("act", lambda e: e.activation(out=HE[:].rearrange("p h t -> p (h t)"), in_=PS[0][:, 0:512], func=AF.Exp),
                  reads=["ps0"], writes=["he"])
            kb.op("act", lambda e: e.activation(out=HE2[:].rearrange("p h t -> p (h t)"), in_=PS[0][:, 0:512], func=AF.Exp, scale=-1.0),
                  reads=["ps0"], writes=["he2"])
            kb.op("dve", lambda e: e.tensor_tensor(out=HQT[:, :, js], in0=HQ[:, :, js], in1=HE[:], op=ALU.mult),
                  reads=["hq", "he"], writes=["hqt"])
            kb.op("dve", lambda e: e.tensor_tensor(out=HKT[:], in0=HKEY[:, :, js], in1=HE2[:], op=ALU.mult),
                  reads=["hkey0", "hkey1", "hkey2", "hkey3", "he2"], writes=["hkt"])
            for h in range(4):
                hs = slice(h * 128, (h + 1) * 128)
                kb.op("pe", lambda e: e.matmul(PS[2][:, hs], lhsT=HKT[:, h, :], rhs=HQT[:, h, js], start=True, stop=True),
                      reads=["hkt", "hqt"], writes=["ps2"], acc=True)
            kb.op("dve", lambda e: e.tensor_tensor(out=HSC[:], in0=PS[2][:, 0:512].rearrange("p (h t) -> p h t", h=4),
                                                   in1=MH_LE.unsqueeze(1).to_broadcast([128, 4, 128]), op=ALU.mult),
                  reads=["ps2", "cst"], writes=["hsc"])
            for h in range(4):
                hs = slice(h * 128, (h + 1) * 128)
                kb.op("pe", lambda e: e.matmul(PS[4 + h][:, js], lhsT=HV[:, j, hs], rhs=HSC[:, h, :], start=True, stop=False),
                      reads=["hv%d" % j, "hsc"], writes=["ps%d" % (4 + h)])
            for c in range(4):
                cs = slice(32 * c, 32 * c + 32)
                for h in range(4):
                    hs = slice(h * 128, (h + 1) * 128)
                    kb.op("pe", lambda e: e.matmul(PS[4 + h][:, j * 128 + 32 * c:j * 128 + 32 * c + 32], lhsT=HSB[:, l, h, :],
                                                   rhs=HQT[:, h, j * 128 + 32 * c:j * 128 + 32 * c + 32], start=False, stop=(c == 3),
                                                   skip_group_check=(c < 3)),
                          reads=["hsb%d" % h, "hqt"], writes=["ps%d" % (4 + h)], acc=True)
                    if c < 3:
                        kb.op("pe", lambda e: e.matmul(PS[h][:, 0:128], lhsT=HKH[cs, hs], rhs=HV[cs, j, hs], start=True, stop=True),
                              reads=["hkh", "hv%d" % j], writes=["ps%d" % h])
                    else:
                        kb.op("pe", lambda e: e.matmul(PS[h][:, 0:128], lhsT=HKH3[64:128, hs], rhs=HV[64:128, j, hs], start=True, stop=True),
                              reads=["hkh3", "hv%d" % j], writes=["ps%d" % h])
                    kb.op("dve", lambda e: e.scalar_tensor_tensor(out=HS[:, l, h, :], in0=HS[:, l, h, :], scalar=HE[:, h, 32 * c + 31:32 * c + 32],
                                                                  in1=PS[h][:, 0:128], op0=ALU.mult, op1=ALU.add),
                          reads=["hs%d" % h, "he", "ps%d" % h], writes=["hs%d" % h])
                    kb.op("act", lambda e: e.copy(out=HSB[:, l, h, :], in_=HS[:, l, h, :]), reads=["hs%d" % h], writes=["hsb%d" % h])
        for h in range(4):
            pb = h % 2
            kb.op("act", lambda e: e.activation(out=HSQ[:].bitcast(BF16)[:, 0:TT], in_=bank(4 + h), func=AF.Square), reads=["ps%d" % (4 + h)], writes=["hsq"])
            kb.op("pe", lambda e: e.matmul(bank(pb), lhsT=ONEB[:], rhs=HSQ[:].bitcast(BF16)[:, 0:TT], start=True, stop=True), reads=["oneb", "hsq"], writes=["ps%d" % pb])
            kb.op("act", lambda e: e.activation(out=HR[:], in_=bank(pb), func=AF.Ln, scale=1.0 / 128, bias=1e-6), reads=["ps%d" % pb], writes=["hr"])
            kb.op("act", lambda e: e.activation(out=HR[:], in_=HR[:], func=AF.Exp, scale=-0.5), reads=["hr"], writes=["hr"])
            kb.op("dve", lambda e: e.scalar_tensor_tensor(out=BR[3][:, h, :], in0=bank(4 + h), scalar=pv(l, "hnorm", h), in1=HR[:],
                                                          op0=ALU.mult, op1=ALU.mult),
                  reads=["ps%d" % (4 + h), "pv", "hr"], writes=["br3"])


    RCARRY = sb("rcarry", [128, DEPTH, 14])
    kb.op("pool", lambda e: e.memset(RCARRY[:], 0.0), writes=["rcarry"])
    RS = sb("rs", [128, DEPTH, 4, 64])
    RSB = sb("rsb", [128, DEPTH, 4, 64], BF16)
    kb.op("pool", lambda e: e.memset(RS[:], 0.0), writes=["rs0", "rs1", "rs2", "rs3"])
    kb.op("pool", lambda e: e.memset(RSB[:], 0.0), writes=["rsb0", "rsb1", "rsb2", "rsb3"])
    NQ = 2 * NJ
    ARr = sb("ar_r", [128, NJ, 2, 128], RDT)
    BTr = sb("bt_r", [128, TT], RDT)
    BT = BTr
    KTr = sb("kt_r", [128, TT], RDT)
    KT = KTr
    BTMr = sb("btm_r", [128, NJ, 128], RDT)
    BTM = BTMr
    KTMr = sb("ktm_r", [128, NJ, 128], RDT)
    KTM = KTMr
    VTMr = sb("vtm_r", [128, NJ, 128], RDT)
    VTM = VTMr
    RPr = [[sb("rp%d_%d_r" % (q, i), [128, 128], RDT) for i in range(2)] for q in range(NQ)]
    RP = RPr
    QT = [sb("qt%d" % q, [128, 7, 128], RDT) for q in range(NQ)]
    RZ2 = sb("rz2", [128, 128], RDT)
    RU2 = sb("ru2", [128, 128], RDT)
    prr = [0]
    evr = [0]

    def rbank():
        i = prr[0]
        prr[0] = (i + 1) % 5
        return i

    def rwkv_stage(l, it):
        arena_reset()
        RD = ab("rd", [128, TT])
        XL = ab("xl", [128, 14, TT])
        TWLA = ab("twla", [128, TT], BF16)
        SGL = ab("sgl", [128, TT], BF16)
        SIGD = ab("sigd", [128, TT])
        A_ = ab("a_", [128, TT])
        GATE = ab("gate", [128, TT])
        KKt = ab("kkt", [128, TT])
        SQ = ab("sq", [128, TT])
        RN = ab("rn", [128, TT])
        KM = ab("km", [128, TT])
        BON = ab("bon", [128, TT])
        CS = ab("cs", [128, TT])
        EIN = ab("ein", [128, TT])
        ENEG = ab("eneg", [128, TT])
        EEX = ab("eex", [128, TT])
        DCOL = ab("dcol", [128, NJ])
        RTT = ab("rtt", [128, 64])
        YS = ab("ys", [128, TT])
        YC = ab("yc", [128, TT])
        RRAW = [ab("rraw%d" % i, [128, 1 + TT]) for i in range(2)]
        for grp in range(4):
            nch = 4 if grp < 3 else 2
            wv, wk = load_w("in", l, wview("in", l, 0, 8, C_B + grp * 512, nch * 128), 8, nch * 128)
            for q in range(nch):
                ci = grp * 4 + q
                pb = rbank()
                proj_fm(l, wv, wk, q, bank(pb), "ps%d" % pb)
                raw = RRAW[ci % 2]
                rk = "rraw%d" % (ci % 2)
                kb.op("pool", lambda e: e.tensor_copy(out=raw[:, 0:1], in_=RCARRY[:, l, ci:ci + 1]), reads=["rcarry"], writes=[rk])
                kb.op("act", lambda e: e.copy(out=raw[:, 1:1 + TT], in_=bank(pb)), reads=["ps%d" % pb], writes=[rk])
                kb.op("pool", lambda e: e.tensor_copy(out=RCARRY[:, l, ci:ci + 1], in_=raw[:, TT:TT + 1]), reads=[rk], writes=["rcarry"])
                kb.op("dve", lambda e: e.tensor_tensor(out=RD[:], in0=raw[:, 0:TT], in1=raw[:, 1:1 + TT], op=ALU.subtract),
                      reads=[rk], writes=["rd"])
                kb.op("dve", lambda e: e.scalar_tensor_tensor(out=XL[:, ci, :], in0=RD[:], scalar=pv(l, "mu", ci), in1=raw[:, 1:1 + TT],
                                                              op0=ALU.mult, op1=ALU.add), reads=["rd", rk, "pv"], writes=["xl%d" % ci])
        kb.op("act", lambda e: e.activation(out=TWLA[0:64, :], in_=XL[0:64, 12, :], func=AF.Tanh), reads=["xl12"], writes=["twla"])
        kb.op("pool", lambda e: e.tensor_copy(out=TWLA[64:128, :], in_=XL[64:128, 12, :]), reads=["xl12"], writes=["twla"])
        kb.op("act", lambda e: e.activation(out=SGL[:], in_=XL[:, 13, :], func=AF.Sigmoid), reads=["xl13"], writes=["sgl"])
        for c in range(4):
            cs_ = slice(c * 128, (c + 1) * 128)
            rkey, kkey, vkey = "xl%d" % c, "xl%d" % (4 + c), "xl%d" % (8 + c)
            r_, k_, v_ = XL[:, c, :], XL[:, 4 + c, :], XL[:, 8 + c, :]
            pb = rbank()
            kb.op("pe", lambda e: e.matmul(bank(pb), lhsT=W2A2[0:64, l, cs_], rhs=TWLA[0:64, :], start=True, stop=True),
                  reads=["w2a2", "twla"], writes=["ps%d" % pb])
            kb.op("act", lambda e: e.activation(out=SIGD[:], in_=bank(pb), func=AF.Sigmoid, bias=pv(l, "w0", c)),
                  reads=["ps%d" % pb, "pv"], writes=["sigd"])
            pb = rbank()
            kb.op("pe", lambda e: e.matmul(bank(pb), lhsT=W2A2[64:128, l, cs_], rhs=TWLA[64:128, :], start=True, stop=True),
                  reads=["w2a2", "twla"], writes=["ps%d" % pb])
            kb.op("act", lambda e: e.activation(out=A_[:], in_=bank(pb), func=AF.Sigmoid, bias=pv(l, "a0", c)),
                  reads=["ps%d" % pb, "pv"], writes=["a_"])
            pb = rbank()
            kb.op("pe", lambda e: e.matmul(bank(pb), lhsT=G2[:, l, cs_], rhs=SGL[:], start=True, stop=True),
                  reads=["g2", "sgl"], writes=["ps%d" % pb])
            kb.op("act", lambda e: e.copy(out=GATE[:], in_=bank(pb)), reads=["ps%d" % pb], writes=["gate"])
            kb.op("dve", lambda e: e.tensor_scalar(out=KKt[:], in0=k_, scalar1=pv(l, "kk", c), scalar2=None, op0=ALU.mult),
                  reads=[kkey, "pv"], writes=["kkt"])
            kb.op("act", lambda e: e.activation(out=SQ[:].bitcast(BF16)[:, 0:TT], in_=KKt[:], func=AF.Square), reads=["kkt"], writes=["sq"])
            pb = rbank()
            kb.op("pe", lambda e: e.matmul(bank(pb), lhsT=BLKB[:], rhs=SQ[:].bitcast(BF16)[:, 0:TT], start=True, stop=True), reads=["blkb", "sq"], writes=["ps%d" % pb])
            kb.op("act", lambda e: e.activation(out=RN[:], in_=bank(pb), func=AF.Ln, bias=1e-24), reads=["ps%d" % pb], writes=["rn"])
            kb.op("act", lambda e: e.activation(out=RN[:], in_=RN[:], func=AF.Exp, scale=-0.5), reads=["rn"], writes=["rn"])
            kb.op("dve", lambda e: e.tensor_tensor(out=KKt[:], in0=KKt[:], in1=RN[:], op=ALU.mult), reads=["kkt", "rn"], writes=["kkt"])
            kb.op("dve", lambda e: e.tensor_scalar(out=KM[:], in0=A_[:], scalar1=pv(l, "ka", c), scalar2=OMKA[:, l, c:c + 1],
                                                   op0=ALU.mult, op1=ALU.add), reads=["a_", "pv", "omka"], writes=["km"])
            kb.op("dve", lambda e: e.tensor_tensor(out=KM[:], in0=KM[:], in1=k_, op=ALU.mult), reads=["km", kkey], writes=["km"])
            kb.op("dve", lambda e: e.scalar_tensor_tensor(out=SQ[:].bitcast(BF16)[:, 0:TT], in0=r_, scalar=pv(l, "rk", c), in1=KM[:], op0=ALU.mult, op1=ALU.mult),
                  reads=[rkey, "pv", "km"], writes=["sq"])
            pb = rbank()
            kb.op("pe", lambda e: e.matmul(bank(pb), lhsT=BLKB[:], rhs=SQ[:].bitcast(BF16)[:, 0:TT], start=True, stop=True), reads=["blkb", "sq"], writes=["ps%d" % pb])
            kb.op("dve", lambda e: e.tensor_tensor(out=BON[:], in0=bank(pb), in1=v_, op=ALU.mult), reads=["ps%d" % pb, vkey], writes=["bon"])
            for j in range(NJ):
                js = slice(j * 128, (j + 1) * 128)
                kb.op("dve", lambda e: e.tensor_tensor_scan(out=CS[:, js], data0=ONES, data1=SIGD[:, js], initial=0.0,
                                                            op0=ALU.mult, op1=ALU.add), reads=["cst", "sigd"], writes=["cs"])
            kb.op("act", lambda e: e.activation(out=EIN[:], in_=CS[:], func=AF.Exp, scale=-LAMB), reads=["cs"], writes=["ein"])
            kb.op("act", lambda e: e.activation(out=ENEG[:], in_=CS[:], func=AF.Exp, scale=LAMB), reads=["cs"], writes=["eneg"])
            kb.op("dve", lambda e: e.tensor_tensor(out=EEX[:], in0=CS[:], in1=SIGD[:], op=ALU.subtract), reads=["cs", "sigd"], writes=["eex"])
            kb.op("act", lambda e: e.activation(out=EEX[:], in_=EEX[:], func=AF.Exp, scale=-LAMB), reads=["eex"], writes=["eex"])
            sub = lambda v: v.rearrange("p (j t) -> p j t", j=NJ)
            kb.op("dve", lambda e: e.scalar_tensor_tensor(out=ARr[:, :, 0, :], in0=sub(KKt[:]), scalar=-1.0, in1=sub(EEX[:]), op0=ALU.mult, op1=ALU.mult),
                  reads=["kkt", "eex"], writes=["at_"])
            kb.op("pool", lambda e: e.tensor_tensor(out=RN[:], in0=KKt[:], in1=A_[:], op=ALU.mult), reads=["kkt", "a_"], writes=["rn"])
            kb.op("dve", lambda e: e.tensor_tensor(out=BTr[:], in0=RN[:], in1=ENEG[:], op=ALU.mult), reads=["rn", "eneg"], writes=["bt"])
            kb.op("dve", lambda e: e.tensor_tensor(out=KTr[:], in0=KM[:], in1=ENEG[:], op=ALU.mult), reads=["km", "eneg"], writes=["kt"])
            kb.op("dve", lambda e: e.tensor_tensor(out=ARr[:, :, 1, :], in0=sub(r_), in1=sub(EIN[:]), op=ALU.mult), reads=[rkey, "ein"], writes=["rt"])
            for j in range(NJ):
                kb.op("pool", lambda e: e.tensor_copy(out=DCOL[:, j:j + 1], in_=EIN[:, j * 128 + 127:j * 128 + 128]), reads=["ein"], writes=["dcol"])
            for j in range(NJ):
                js = slice(j * 128, (j + 1) * 128)
                for src, skey, dst, dkey, isb in ((BT, "bt", BTMr, "btm", True), (KT, "kt", KTMr, "ktm", True),
                                                  (XL[:, 8 + c, :], vkey, VTMr, "vtm", False)):
                    pb = rbank()
                    sap = src[:, js]
                    po = PS[pb][:].bitcast(BF16)[:, 0:128] if isb else PS[pb][:, 0:128]
                    idn = IDB[:] if isb else IDENT
                    kb.op("pe", lambda e: e.transpose(out=po, in_=sap, identity=idn), reads=[skey, "cst", "idb"], writes=["ps%d" % pb])
                    kb.op("act", lambda e: e.copy(out=dst[:, j, :], in_=po), reads=["ps%d" % pb], writes=[dkey])
            combos = [(h, j) for j in range(NJ) for h in range(2)]

            def qi(h, j):
                return h * NJ + j

            def mm_ev(q, lhsT, rhs, lk, rk_, dst, dkey, mask=None, eng="dve", dst_in=None):
                pb = rbank()
                kb.op("pe", lambda e: e.matmul(PS[pb][:, 0:128], lhsT=lhsT, rhs=rhs, start=True, stop=True), reads=lk + rk_, writes=["ps%d" % pb])
                if mask is not None:
                    kb.op("dve", lambda e: e.tensor_tensor(out=dst[:], in0=PS[pb][:, 0:128], in1=mask, op=ALU.mult),
                          reads=["ps%d" % pb, "cst"], writes=[dkey])
                elif eng == "add":
                    kb.op("dve", lambda e: e.tensor_tensor(out=dst[:], in0=PS[pb][:, 0:128], in1=dst_in[:], op=ALU.add),
                          reads=["ps%d" % pb, dkey], writes=[dkey])
                else:
                    kb.op("act", lambda e: e.copy(out=dst[:], in_=PS[pb][:, 0:128]), reads=["ps%d" % pb], writes=[dkey])

            MASK2 = CST[:, 128:384].rearrange("p (a t) -> p a t", a=2)
            for (h, j) in combos:
                q = qi(h, j)
                hp = slice(64 * h, 64 * h + 64)
                js = slice(j * 128, (j + 1) * 128)
                for lhs_, lk_, o0, o1, dk in ((BTr, "bt", 0, 2, "qtA%d" % q), (KTr, "kt", 3, 4, "qtK%d" % q)):
                    pb = rbank()
                    kb.op("pe", lambda e: e.matmul(PS[pb][:, 0:256], lhsT=lhs_[hp, js], rhs=ARr[hp, j, :, :], start=True, stop=True),
                          reads=[lk_, "at_", "rt"], writes=["ps%d" % pb])
                    oap = QT[q][:, o0:o1 + 1:(o1 - o0), :]
                    kb.op("dve", lambda e: e.tensor_tensor(out=oap, in0=PS[pb][:, 0:256].rearrange("p (a t) -> p a t", a=2), in1=MASK2, op=ALU.mult),
                          reads=["ps%d" % pb, "cst"], writes=[dk])
                mm_ev(q, ARr[hp, j, 0, :], BTr[hp, js], ["at_"], ["bt"], RPr[q][0], "rp%d_0" % q, mask=M_GT)
                kb.op("pool", lambda e: e.tensor_tensor(out=QT[q][:, 6, :], in0=QT[q][:, 0, :], in1=IDENT, op=ALU.add),
                      reads=["qtA%d" % q, "cst"], writes=["qtR%d_1" % q])
            WS = (0, 5)
            for k in range(0, 7):
                x, y = k % 2, (k + 1) % 2
                for (h, j) in combos:
                    q = qi(h, j)
                    Wx, Wy = WS[x], WS[y]
                    ptkey = "qtA%d" % q if k == 0 else "qtP%d_%d" % (q, x)
                    if k <= 5:
                        mm_ev(q, QT[q][:, Wx, :], RPr[q][x][:], [ptkey], ["rp%d_%d" % (q, x)], RPr[q][y], "rp%d_%d" % (q, y))
                    if k == 0:
                        mm_ev(q, RPr[q][0][:], QT[q][:, Wx, :], ["rp%d_0" % q], [ptkey], QT[q][:, Wy, :], "qtP%d_%d" % (q, y))
                    elif k <= 4:
                        pb = rbank()
                        kb.op("pe", lambda e: e.matmul(PS[pb][:, 0:256], lhsT=RPr[q][x][:], rhs=QT[q][:, Wx:Wx + 2, :], start=True, stop=True),
                              reads=["rp%d_%d" % (q, x), ptkey, "qtR%d_%d" % (q, x)], writes=["ps%d" % pb])
                        kb.op("dve", lambda e: e.tensor_tensor(out=QT[q][:, Wy + 1, :], in0=PS[pb][:, 128:256], in1=QT[q][:, Wx + 1, :], op=ALU.add),
                              reads=["ps%d" % pb, "qtR%d_%d" % (q, x)], writes=["qtR%d_%d" % (q, y)])
                        kb.op("dve", lambda e: e.tensor_copy(out=QT[q][:, Wy, :], in_=PS[pb][:, 0:128]),
                              reads=["ps%d" % pb], writes=["qtP%d_%d" % (q, y)] + (["qtA%d" % q] if Wy == 0 else []))
                    else:
                        pb = rbank()
                        kb.op("pe", lambda e: e.matmul(PS[pb][:, 0:128], lhsT=RPr[q][x][:], rhs=QT[q][:, Wx + 1, :], start=True, stop=True),
                              reads=["rp%d_%d" % (q, x), "qtR%d_%d" % (q, x)], writes=["ps%d" % pb])
                        kb.op("dve", lambda e: e.tensor_tensor(out=QT[q][:, Wy + 1, :], in0=PS[pb][:, 0:128], in1=QT[q][:, Wx + 1, :], op=ALU.add),
                              reads=["ps%d" % pb, "qtR%d_%d" % (q, x)], writes=["qtR%d_%d" % (q, y)])
            RFIN = WS[7 % 2] + 1
            RFK = "qtR%%d_%d" % (7 % 2)
            for j in range(NJ):
                js = slice(j * 128, (j + 1) * 128)
                for h in range(2):
                    q = qi(h, j)
                    hp = slice(64 * h, 64 * h + 64)
                    zo = PS[7][:, h * 64:h * 64 + 64]
                    kb.op("pe", lambda e: e.matmul(zo, lhsT=ARr[hp, j, 0, :], rhs=RSB[hp, l, c, :], start=True, stop=False),
                          reads=["at_", "rsb%d" % c], writes=["ps7"], acc=(h == 1))
                    kb.op("pe", lambda e: e.matmul(zo, lhsT=QT[q][:, 3, :], rhs=VTMr[:, j, hp], start=False, stop=True),
                          reads=["qtK%d" % q, "vtm"], writes=["ps7"], acc=True)
                    if h == 1:
                        kb.op("act", lambda e: e.copy(out=RZ2[:], in_=PS[7][:, 0:128]), reads=["ps7"], writes=["rz"])
                for h in range(2):
                    q = qi(h, j)
                    uo = PS[7][:, 128 + h * 64:128 + h * 64 + 64]
                    kb.op("pe", lambda e: e.matmul(uo, lhsT=QT[q][:, RFIN, :], rhs=RZ2[:, h * 64:h * 64 + 64], start=True, stop=True),
                          reads=[RFK % q, "rz"], writes=["ps7"], acc=(h == 1))
                    if h == 1:
                        kb.op("act", lambda e: e.copy(out=RU2[:], in_=PS[7][:, 128:256]), reads=["ps7"], writes=["ru"])
                for h in range(2):
                    q = qi(h, j)
                    hp = slice(64 * h, 64 * h + 64)
                    yo = PS[6][hp, js]
                    kb.op("pe", lambda e: e.matmul(yo, lhsT=RSB[hp, l, c, :], rhs=ARr[hp, j, 1, :], start=True, stop=False),
                          reads=["rsb%d" % c, "rt"], writes=["ps6"], acc=True)
                    kb.op("pe", lambda e: e.matmul(yo, lhsT=RU2[:, h * 64:h * 64 + 64], rhs=QT[q][:, 2, :], start=False, stop=False),
                          reads=["ru", "qtA%d" % q], writes=["ps6"], acc=True)
                    kb.op("pe", lambda e: e.matmul(yo, lhsT=VTM[:, j, hp], rhs=QT[q][:, 4, :], start=False, stop=True),
                          reads=["vtm", "qtK%d" % q], writes=["ps6"], acc=True)
                for h in range(2):
                    hp = slice(64 * h, 64 * h + 64)
                    to = PS[5][hp, 0:64]
                    kb.op("pe", lambda e: e.matmul(to, lhsT=BTM[:, j, hp], rhs=RU2[:, h * 64:h * 64 + 64], start=True, stop=False),
                          reads=["btm", "ru"], writes=["ps5"], acc=(h == 1))
                    kb.op("pe", lambda e: e.matmul(to, lhsT=KTM[:, j, hp], rhs=VTM[:, j, hp], start=False, stop=True),
                          reads=["ktm", "vtm"], writes=["ps5"], acc=True)
                kb.op("dve", lambda e: e.tensor_tensor(out=RTT[:], in0=PS[5][:, 0:64], in1=RS[:, l, c, :], op=ALU.add),
                      reads=["ps5", "rs%d" % c], writes=["rtt"])
                kb.op("dve", lambda e: e.tensor_scalar(out=RS[:, l, c, :], in0=RTT[:], scalar1=DCOL[:, j:j + 1], scalar2=None, op0=ALU.mult),
                      reads=["rtt", "dcol"], writes=["rs%d" % c])
                kb.op("act", lambda e: e.copy(out=RSB[:, l, c, :], in_=RS[:, l, c, :]), reads=["rs%d" % c], writes=["rsb%d" % c])
            kb.op("act", lambda e: e.copy(out=YS[:], in_=bank(6)), reads=["ps6"], writes=["ys"])
            pb = rbank()
            kb.op("pe", lambda e: e.matmul(bank(pb), lhsT=BLK64, rhs=YS[:], start=True, stop=True), reads=["cst", "ys"], writes=["ps%d" % pb])
            kb.op("dve", lambda e: e.scalar_tensor_tensor(out=YC[:], in0=bank(pb), scalar=-1.0 / 64, in1=YS[:], op0=ALU.mult, op1=ALU.add),
                  reads=["ps%d" % pb, "ys"], writes=["yc"])
            kb.op("act", lambda e: e.activation(out=YS[:].bitcast(BF16)[:, 0:TT], in_=YC[:], func=AF.Square), reads=["yc"], writes=["ys"])
            pb = rbank()
            kb.op("pe", lambda e: e.matmul(bank(pb), lhsT=BLKB[:], rhs=YS[:].bitcast(BF16)[:, 0:TT], start=True, stop=True), reads=["blkb", "ys"], writes=["ps%d" % pb])
            kb.op("act", lambda e: e.activation(out=YS[:], in_=bank(pb), func=AF.Ln, scale=1.0 / 64, bias=64e-5), reads=["ps%d" % pb], writes=["ys"])
            kb.op("act", lambda e: e.activation(out=YS[:], in_=YS[:], func=AF.Exp, scale=-0.5), reads=["ys"], writes=["ys"])
            kb.op("dve", lambda e: e.tensor_tensor(out=YC[:], in0=YC[:], in1=YS[:], op=ALU.mult), reads=["yc", "ys"], writes=["yc"])
            kb.op("dve", lambda e: e.tensor_scalar(out=YC[:], in0=YC[:], scalar1=pv(l, "ln_w", c), scalar2=pv(l, "ln_b", c),
                                                   op0=ALU.mult, op1=ALU.add), reads=["yc", "pv"], writes=["yc"])
            kb.op("pool", lambda e: e.tensor_tensor(out=YC[:], in0=YC[:], in1=BON[:], op=ALU.add), reads=["yc", "bon"], writes=["yc"])
            kb.op("dve", lambda e: e.tensor_tensor(out=BR[1][:, c, :], in0=YC[:], in1=GATE[:], op=ALU.mult), reads=["yc", "gate"], writes=["br1"])

    def dump_br(l, it):
        for k in range(4):
            for c in range(4):
                kb.dma("pool", dbg_br[l, k, c, :, it * TT:(it + 1) * TT], BR[k][:, c, :], reads=["br%d" % k], writes=["dbg"])

    def dump_x(l, which, it):
        for j in range(NJ):
            kb.dma("sp", dbg_x[l, which, it * TT + j * 128: it * TT + (j + 1) * 128, :], X[:, j, :], reads=["X%d" % j], writes=["dbg"])

    for it in range(NT):
        for j in range(NJ):
            kb.dma("sp", X[:, j, :], x_d[it * TT + j * 128: it * TT + (j + 1) * 128, :], writes=["X%d" % j])
        for l in range(nlayers):
            rmsnorm_T(l, "g_mix", None)
            if "pool" in stages:
                pool_stage(l, it)
            else:
                zero_branch(0)
            if "rwkv" in stages:
                rwkv_stage(l, it)
            else:
                zero_branch(1)
            if "sg" in stages:
                sg_stage(l, it)
            else:
                zero_branch(2)
            if "hgrn" in stages:
                hgrn_stage(l, it)
            else:
                zero_branch(3)
            if debug:
                dump_br(l, it)
            merge_stage(l, it)
            if debug:
                dump_x(l, 0, it)
            if "ffn" in stages:
                ffn_stage(l, it)
            if debug:
                dump_x(l, 1, it)
            if "ple" in stages:
                ple_stage(l, it)
            if debug:
                dump_x(l, 2, it)
        gfin = BC[:, DEPTH * BR_W:DEPTH * BR_W + 1024]
        for j in range(NJ):
            kb.op("act", lambda e: e.activation(out=XN[j % 2][:], in_=X[:, j, :], func=AF.Square, accum_out=SS[:, j:j + 1]),
                  reads=["X%d" % j], writes=["xn%d" % (j % 2), "ss%d" % j])
            kb.op("act", lambda e: e.activation(out=SS[:, j:j + 1], in_=SS[:, j:j + 1], func=AF.Ln, scale=1.0 / D, bias=1e-6),
                  reads=["ss%d" % j], writes=["ss%d" % j])
            kb.op("act", lambda e: e.activation(out=SS[:, j:j + 1], in_=SS[:, j:j + 1], func=AF.Exp, scale=-0.5),
                  reads=["ss%d" % j], writes=["ss%d" % j])
            for hf in range(2):
                hsl = slice(hf * 512, (hf + 1) * 512)
                kb.op("dve", lambda e: e.scalar_tensor_tensor(out=TMP[hf][:], in0=X[:, j, hsl], scalar=SS[:, j:j + 1], in1=gfin[:, hsl],
                                                              op0=ALU.mult, op1=ALU.mult),
                      reads=["X%d" % j, "ss%d" % j, "bc"], writes=["tmp%d" % hf])
                kb.dma("sp", out_d[it * TT + j * 128: it * TT + (j + 1) * 128, hsl], TMP[hf][:], reads=["tmp%d" % hf], writes=["out"])
    for e in ("sp", "pool", "act"):
        kb.wait_all(e)
    print("instructions", kb.n_inst, "waits", kb.n_wait, "arena max words", max(amax[0], aoff[0]))
    kb.close()
    return nc


def make_in_maps(inputs, S, ncores):
    cst = make_consts()
    pvv = pack_pv(inputs)
    bc = pack_bc(inputs)
    sm = pack_small(inputs)
    f = lambda a: np.ascontiguousarray(np.asarray(a, np.float32))
    shared = {
        "w_in": f(inputs["w_in"]), "w_branch": f(inputs["w_branch"]).reshape(DEPTH, 4 * MIXW, D), "w_out": f(inputs["w_out"]),
        "ffn_up": f(inputs["ffn_up"]), "ffn_down": f(inputs["ffn_down"]), "ple_proj": f(inputs["ple_proj"]),
        "ple_gate": f(inputs["ple_gate"]), "cst": cst, "pv": pvv, "bc": bc, "sm": sm, "sgb": pack_sgb(inputs),
    }
    maps = []
    for b in range(ncores):
        m = dict(shared)
        m["x"] = f(inputs["x"][b])
        m["p"] = f(inputs["p"][:, b])
        maps.append(m)
    return maps


def kernel(**inputs):
    inputs = {k: np.asarray(v) for k, v in inputs.items()}
    B, S, _ = inputs["x"].shape
    nc = build_nc(S)
    maps = make_in_maps(inputs, S, B)
    res = run_bass_kernel_spmd(nc, maps, core_ids=list(range(B)))
    return np.stack([r["out"] for r in res.results], axis=0).astype(np.float32)
```
